# Optimizing a Trainium2 kernel written in Bass

```python
import jax, jax.numpy as jnp
from jax import lax
import numpy as np

D_MODEL = 1024
BATCH = 32
SEQ = 256
DEPTH = 4
DEC_BATCH = 4
DEC_SEQ = 4096
PAST_LEN = 256

GRID_W = 64
N_MOD = 9
D_FF = 2816
N_CM_LAYERS = (DEPTH + 1) // 2
N_ATTN_LAYERS = DEPTH // 2
CONV_CH = D_MODEL // 2
CONV_WIDTH = 31
SGU_CH = D_MODEL // 2
SGU_GROUPS = 4
CHUNK = 128
SGU_GC = SGU_CH // SGU_GROUPS
HEAD_DIM = 64
N_HEADS = D_MODEL // HEAD_DIM
N_KV_HEADS = 4
GROUP = N_HEADS // N_KV_HEADS
Q_DIM = N_HEADS * HEAD_DIM
KV_DIM = N_KV_HEADS * HEAD_DIM
BLOCK = 128
WINDOW = 128
ROPE_HALF = HEAD_DIM // 2
ROPE_BASE = 10000.0
EPS = 1e-6
NEG_INF = -1e30
ATTN_SCALE = HEAD_DIM ** -0.5

kernel_name = "hybrid_dit_conv_gmlp_swa_step"


def rms_norm(x, g):
    xf = x.astype(jnp.float32)
    y = xf * lax.rsqrt(jnp.mean(xf * xf, axis=-1, keepdims=True) + EPS)
    return (y * g.astype(jnp.float32)).astype(x.dtype)


def layer_norm(x, g, b):
    xf = x.astype(jnp.float32)
    mu = jnp.mean(xf, axis=-1, keepdims=True)
    var = jnp.mean(jnp.square(xf - mu), axis=-1, keepdims=True)
    y = (xf - mu) * lax.rsqrt(var + EPS)
    return (y * g.astype(jnp.float32) + b.astype(jnp.float32)).astype(x.dtype)


def modulation(cond, w_mod, b_mod):
    m = jax.nn.silu(cond) @ w_mod + b_mod
    return m.reshape(cond.shape[0], N_MOD, D_MODEL)


def pre(x, m, g, slot):
    h = rms_norm(x, g)
    return h * (1 + m[:, 3 * slot + 1][:, None, :]) + m[:, 3 * slot][:, None, :]


def residual(x, y, m, g, slot, weight):
    return x + weight * m[:, 3 * slot + 2][:, None, :] * rms_norm(y, g)


def swiglu(h, wg, wu, wd):
    return (jax.nn.silu(h @ wg) * (h @ wu)) @ wd


def conv_gmlp(h, w_in, conv_w, conv_b, cln_g, cln_b, sln_g, sln_b, sgu_w, sgu_b, w_out):
    B, T, _ = h.shape
    z = h @ w_in
    a_val, a_gate, u, v = jnp.split(z, [CONV_CH, 2 * CONV_CH, 2 * CONV_CH + SGU_CH], axis=-1)
    g = a_val * jax.nn.sigmoid(a_gate)
    g = lax.conv_general_dilated(
        g, conv_w[:, None, :], window_strides=(1,),
        padding=[(CONV_WIDTH // 2, CONV_WIDTH // 2)],
        dimension_numbers=('NWC', 'WIO', 'NWC'),
        feature_group_count=CONV_CH) + conv_b
    a_out = jax.nn.silu(layer_norm(g, cln_g, cln_b))
    u = jax.nn.gelu(u)
    v = layer_norm(jax.nn.gelu(v), sln_g, sln_b)
    vc = v.reshape(B, T // CHUNK, CHUNK, SGU_GROUPS, SGU_GC)
    vc = jnp.einsum('gpq,bnqgc->bnpgc', sgu_w, vc) + sgu_b.T[:, :, None]
    b_out = u * vc.reshape(B, T, SGU_CH)
    return jnp.concatenate([a_out, b_out], axis=-1) @ w_out


def split_qkv(z):
    B, T, _ = z.shape
    q = z[..., :Q_DIM].reshape(B, T, N_KV_HEADS, GROUP, HEAD_DIM)
    k = z[..., Q_DIM:Q_DIM + KV_DIM].reshape(B, T, N_KV_HEADS, HEAD_DIM)
    v = z[..., Q_DIM + KV_DIM:].reshape(B, T, N_KV_HEADS, HEAD_DIM)
    return q, k, v


def axial_angles(T):
    rows = T // GRID_W
    r = jnp.repeat(jnp.arange(rows), GRID_W).astype(jnp.float32)
    col = jnp.tile(jnp.arange(GRID_W), rows).astype(jnp.float32)
    inv_freq = jnp.power(ROPE_BASE, -jnp.arange(0, ROPE_HALF, 2, dtype=jnp.float32) / ROPE_HALF)
    return r[:, None] * inv_freq[None, :], col[:, None] * inv_freq[None, :]


def rope_half(xp, ang):
    T, F = ang.shape
    shp = (1, T) + (1,) * (xp.ndim - 3) + (F,)
    cos = jnp.cos(ang).reshape(shp).astype(xp.dtype)
    sin = jnp.sin(ang).reshape(shp).astype(xp.dtype)
    a, b = xp[..., :F], xp[..., F:]
    return jnp.concatenate([a * cos - b * sin, b * cos + a * sin], axis=-1)


def axial_rope(x, ang_r, ang_c):
    return jnp.concatenate([rope_half(x[..., :ROPE_HALF], ang_r),
                            rope_half(x[..., ROPE_HALF:], ang_c)], axis=-1)


def sink_softmax(s, sink):
    col = jnp.broadcast_to(sink.astype(jnp.float32).reshape(1, N_KV_HEADS, GROUP, 1, 1),
                           s.shape[:-1] + (1,))
    p = jax.nn.softmax(jnp.concatenate([s, col], axis=-1), axis=-1)
    return p[..., :-1]


def context_attention(q, k, v, sink):
    B, L = q.shape[:2]

    def one_block(i):
        qb = lax.dynamic_slice_in_dim(q, i * BLOCK, BLOCK, axis=1)
        s = jnp.einsum('bqkgd,bskd->bkgqs', qb, k).astype(jnp.float32) * ATTN_SCALE
        p = sink_softmax(s, sink).astype(v.dtype)
        return jnp.einsum('bkgqs,bskd->bqkgd', p, v)

    o = lax.map(one_block, jnp.arange(L // BLOCK))
    return jnp.moveaxis(o, 0, 1).reshape(B, L, Q_DIM)


def latent_attention(q, k, v, k_ctx, v_ctx, sink):
    B, T = q.shape[:2]
    pad = ((0, 0), (BLOCK, BLOCK), (0, 0), (0, 0))
    kpad = jnp.pad(k, pad)
    vpad = jnp.pad(v, pad)

    def one_block(i):
        start = i * BLOCK
        qb = lax.dynamic_slice_in_dim(q, start, BLOCK, axis=1)
        kb = lax.dynamic_slice_in_dim(kpad, start, 3 * BLOCK, axis=1)
        vb = lax.dynamic_slice_in_dim(vpad, start, 3 * BLOCK, axis=1)
        qpos = start + jnp.arange(BLOCK)
        kpos = start - BLOCK + jnp.arange(3 * BLOCK)
        valid = ((jnp.abs(qpos[:, None] - kpos[None, :]) <= WINDOW)
                 & (kpos[None, :] >= 0) & (kpos[None, :] < T))
        s_loc = jnp.einsum('bqkgd,bskd->bkgqs', qb, kb).astype(jnp.float32) * ATTN_SCALE
        s_loc = jnp.where(valid, s_loc, NEG_INF)
        s_ctx = jnp.einsum('bqkgd,bskd->bkgqs', qb, k_ctx).astype(jnp.float32) * ATTN_SCALE
        p = sink_softmax(jnp.concatenate([s_loc, s_ctx], axis=-1), sink).astype(v.dtype)
        return (jnp.einsum('bkgqs,bskd->bqkgd', p[..., :3 * BLOCK], vb)
                + jnp.einsum('bkgqs,bskd->bqkgd', p[..., 3 * BLOCK:], v_ctx))

    o = lax.map(one_block, jnp.arange(T // BLOCK))
    return jnp.moveaxis(o, 0, 1).reshape(B, T, Q_DIM)


def setup_inputs(seed: int = 0) -> dict:
    key = jax.random.key(seed)
    ks = iter(jax.random.split(key, 32))
    f32 = jnp.float32

    def nrm(shape, scale=1.0):
        return jax.random.normal(next(ks), shape, f32) * scale

    return {
        "x_prompt": nrm((BATCH, SEQ, D_MODEL)),
        "x_sample": nrm((DEC_BATCH, DEC_SEQ, D_MODEL)),
        "cache_k": nrm((DEC_BATCH, N_ATTN_LAYERS, PAST_LEN, N_KV_HEADS, HEAD_DIM)),
        "cache_v": nrm((DEC_BATCH, N_ATTN_LAYERS, PAST_LEN, N_KV_HEADS, HEAD_DIM)),
        "c": nrm((DEC_BATCH, D_MODEL)),
        "c_ctx": nrm((D_MODEL,)),
        "w_mod": nrm((DEPTH, D_MODEL, N_MOD * D_MODEL), D_MODEL ** -0.5),
        "b_mod": nrm((DEPTH, N_MOD * D_MODEL), 0.02),
        "norm_w": 1.0 + nrm((DEPTH, 6, D_MODEL), 0.02),
        "ffn_w_gate": nrm((DEPTH, 2, D_MODEL, D_FF), D_MODEL ** -0.5),
        "ffn_w_up": nrm((DEPTH, 2, D_MODEL, D_FF), D_MODEL ** -0.5),
        "ffn_w_down": nrm((DEPTH, 2, D_FF, D_MODEL), D_FF ** -0.5),
        "cm_w_in": nrm((N_CM_LAYERS, D_MODEL, 2 * CONV_CH + 2 * SGU_CH), D_MODEL ** -0.5),
        "cm_conv_w": nrm((N_CM_LAYERS, CONV_WIDTH, CONV_CH), CONV_WIDTH ** -0.5),
        "cm_conv_b": nrm((N_CM_LAYERS, CONV_CH), 0.02),
        "cm_conv_ln_g": 1.0 + nrm((N_CM_LAYERS, CONV_CH), 0.02),
        "cm_conv_ln_b": nrm((N_CM_LAYERS, CONV_CH), 0.02),
        "cm_sgu_ln_g": 1.0 + nrm((N_CM_LAYERS, SGU_CH), 0.02),
        "cm_sgu_ln_b": nrm((N_CM_LAYERS, SGU_CH), 0.02),
        "cm_sgu_w": nrm((N_CM_LAYERS, SGU_GROUPS, CHUNK, CHUNK), CHUNK ** -0.5),
        "cm_sgu_b": nrm((N_CM_LAYERS, SGU_GROUPS, CHUNK), 0.02),
        "cm_w_out": nrm((N_CM_LAYERS, CONV_CH + SGU_CH, D_MODEL), (CONV_CH + SGU_CH) ** -0.5),
        "attn_w_qkv": nrm((N_ATTN_LAYERS, D_MODEL, Q_DIM + 2 * KV_DIM), D_MODEL ** -0.5),
        "attn_w_o": nrm((N_ATTN_LAYERS, Q_DIM, D_MODEL), Q_DIM ** -0.5),
        "attn_sink": nrm((N_ATTN_LAYERS, N_HEADS), 0.5),
    }


def reference(x_prompt, x_sample, cache_k, cache_v, c, c_ctx, w_mod, b_mod, norm_w,
              ffn_w_gate, ffn_w_up, ffn_w_down, cm_w_in, cm_conv_w, cm_conv_b,
              cm_conv_ln_g, cm_conv_ln_b, cm_sgu_ln_g, cm_sgu_ln_b, cm_sgu_w, cm_sgu_b,
              cm_w_out, attn_w_qkv, attn_w_o, attn_sink):
    xp, xs = x_prompt, x_sample
    ang_r, ang_c = axial_angles(xs.shape[1])
    new_k, new_v = [], []
    for l in range(DEPTH):
        mp = modulation(c_ctx[None, :], w_mod[l], b_mod[l])
        ms = modulation(c, w_mod[l], b_mod[l])
        f1 = (ffn_w_gate[l, 0], ffn_w_up[l, 0], ffn_w_down[l, 0])
        xp = residual(xp, swiglu(pre(xp, mp, norm_w[l, 0], 0), *f1), mp, norm_w[l, 1], 0, 0.5)
        xs = residual(xs, swiglu(pre(xs, ms, norm_w[l, 0], 0), *f1), ms, norm_w[l, 1], 0, 0.5)
        hp = pre(xp, mp, norm_w[l, 2], 1)
        hs = pre(xs, ms, norm_w[l, 2], 1)
        j = l // 2
        if l % 2 == 0:
            cm = (cm_w_in[j], cm_conv_w[j], cm_conv_b[j], cm_conv_ln_g[j], cm_conv_ln_b[j],
                  cm_sgu_ln_g[j], cm_sgu_ln_b[j], cm_sgu_w[j], cm_sgu_b[j], cm_w_out[j])
            yp = conv_gmlp(hp, *cm)
            ys = conv_gmlp(hs, *cm)
        else:
            qp, kp, vp = split_qkv(hp @ attn_w_qkv[j])
            yp = context_attention(qp, kp, vp, attn_sink[j].reshape(N_KV_HEADS, GROUP)) @ attn_w_o[j]
            new_k.append(kp)
            new_v.append(vp)
            qs, ks_, vs = split_qkv(hs @ attn_w_qkv[j])
            qs = axial_rope(qs, ang_r, ang_c)
            ks_ = axial_rope(ks_, ang_r, ang_c)
            ys = latent_attention(qs, ks_, vs, cache_k[:, j], cache_v[:, j],
                                  attn_sink[j].reshape(N_KV_HEADS, GROUP)) @ attn_w_o[j]
        xp = residual(xp, yp, mp, norm_w[l, 3], 1, 1.0)
        xs = residual(xs, ys, ms, norm_w[l, 3], 1, 1.0)
        f2 = (ffn_w_gate[l, 1], ffn_w_up[l, 1], ffn_w_down[l, 1])
        xp = residual(xp, swiglu(pre(xp, mp, norm_w[l, 4], 2), *f2), mp, norm_w[l, 5], 2, 0.5)
        xs = residual(xs, swiglu(pre(xs, ms, norm_w[l, 4], 2), *f2), ms, norm_w[l, 5], 2, 0.5)
    return (xp, xs, jnp.stack(new_k, axis=1), jnp.stack(new_v, axis=1))
```

```python
import os
import numpy as np
from contextlib import ExitStack
import concourse.bass as bass
import concourse.mybir as mybir
from concourse.bass_utils import run_bass_kernel_spmd

F32 = mybir.dt.float32
BF16 = mybir.dt.bfloat16
AF = mybir.ActivationFunctionType
ALU = mybir.AluOpType

D = 1024
DFF = 2816
NFC = 22
DEPTH = 4
NTP = 1024
NTS = 2560
NOWN = 2048
EPS = 1e-6
KQ = 8
NWA = 6
NWD = 2
SCRW = 19200
USE_GELU_TANH = False
SHRINK = True
PREPRE = True


class Buf:
    __slots__ = ("name", "w", "r")

    def __init__(self, name):
        self.name = name
        self.w = None
        self.r = []


class Op:
    __slots__ = ("eng", "fn", "deps", "dma", "sig", "sigval", "dn", "idx", "waits")


class Prog:
    def __init__(self):
        self.ops = []
        self.ndma = {"sp": 0, "pool": 0}
        self.dmaops = {"sp": [], "pool": []}
        self.last = {}
        self.pend = {}

    def add(self, eng, fn, reads=(), writes=(), dma=False, exempt=False):
        op = Op()
        op.eng = eng
        op.fn = fn
        op.dma = dma
        op.sig = False
        op.sigval = 0
        op.idx = len(self.ops)
        deps = set()
        for b in reads:
            if b.w is not None:
                deps.add(b.w)
        for b in writes:
            if b.w is not None:
                deps.add(b.w)
            for r in b.r:
                deps.add(r)
        for b in writes:
            b.w = op
            b.r = []
        for b in reads:
            if not dma:
                b.r = [o for o in b.r if o.dma or o.eng != eng]
            b.r.append(op)
        if dma:
            n = self.ndma[eng]
            self.ndma[eng] += 1
            op.dn = n
            if n >= KQ:
                deps.add(self.dmaops[eng][n - KQ])
            self.dmaops[eng].append(op)
        if eng in self.pend and not exempt:
            deps |= self.pend.pop(eng)
        deps.discard(op)
        op.deps = deps
        self.ops.append(op)
        if not dma:
            self.last[eng] = op
        return op

    def barrier(self):
        src = set(self.last.values())
        for q in ("sp", "pool"):
            src |= set(self.dmaops[q][-KQ:])
        for e in ("act", "dve", "pool", "sp"):
            self.pend[e] = set(src) | self.pend.get(e, set())

    def finalize(self):
        src = set()
        for q in ("sp", "pool"):
            src |= set(self.dmaops[q][-KQ:])
        op = self.add("sp", None)
        op.deps |= src
        seen = {}
        for op in self.ops:
            ws = []
            for d in sorted(op.deps, key=lambda o: o.idx):
                if d.dma:
                    key = ("dma", d.eng, d.dn % KQ)
                    val = d.dn // KQ + 1
                else:
                    if d.eng == op.eng and not op.dma and op.eng == "pe":
                        continue
                    key = ("eng", d.eng)
                    val = d.idx
                sk = (op.eng, key)
                if seen.get(sk, -1) >= val:
                    continue
                seen[sk] = val
                ws.append(d)
                if not d.dma:
                    d.sig = True
            op.waits = ws
        cnt = {}
        for op in self.ops:
            if op.sig:
                cnt[op.eng] = cnt.get(op.eng, 0) + 1
                op.sigval = cnt[op.eng]
        return cnt

    def emit(self, engname, eng, esem, dsem):
        for op in self.ops:
            if op.eng != engname:
                continue
            for d in op.waits:
                if d.dma:
                    eng.wait_ge(dsem[d.eng][d.dn % KQ], 16 * (d.dn // KQ + 1))
                else:
                    eng.wait_ge(esem[d.eng], d.sigval)
            if op.fn is None:
                continue
            ins = op.fn(eng)
            if op.dma:
                ins.then_inc(dsem[op.eng][op.dn % KQ], 16)
            elif op.sig:
                ins.then_inc(esem[op.eng], 1)


def build(depth=DEPTH, phases=("P", "S"), do_mixer=True, dbg=False, mix=("conv", "grp", "attn", "att", "op")):
    nc = bass.Bass("TRN2", target_bir_lowering=False)
    P = Prog()

    def din(name, shape):
        return nc.dram_tensor(name, list(shape), F32, kind="ExternalInput").ap()

    def dout(name, shape):
        return nc.dram_tensor(name, list(shape), F32, kind="ExternalOutput").ap()

    xT = {"P": din("xpT", (D, NTP)), "S": din("xsT", (D, NTS))}
    yT = {"P": dout("ypT", (D, NTP)), "S": dout("ysT", (D, NOWN))}
    nk_o = dout("nk", (2, 4, 256, 256))
    nv_o = dout("nv", (2, 4, 256, 256))
    w_mod = din("w_mod", (DEPTH, D, 9 * D))
    w_gate = din("ffn_w_gate", (DEPTH, 2, D, DFF))
    w_up = din("ffn_w_up", (DEPTH, 2, D, DFF))
    w_down = din("ffn_w_down", (DEPTH, 2, DFF, D))
    w_in = din("cm_w_in", (2, D, 2048))
    w_out = din("cm_w_out", (2, D, D))
    w_att = din("w_att", (2, D, 3584))
    w_o = din("attn_w_o", (2, D, D))
    bmodT_d = din("bmodT", (128, 288))
    normwT_d = din("normwT", (128, 192))
    cond_d = din("condT", (128, 16))
    convw_d = {"P": din("convw_p", (2, 128, 124)), "S": din("convw_s", (2, 128, 124))}
    convv_d = din("convv", (2, 128, 12))
    slnv_d = din("slnv", (2, 128, 1024))
    sguw_d = {"P": din("sguw_p", (2, 128, 512)), "S": din("sguw_s", (2, 128, 512))}
    sgub_d = {"P": din("sgub_p", (2, 128, 2048)), "S": din("sgub_s", (2, 128, 2048))}
    sink_d = din("sinkrow", (2, 128, 2048))
    kcT_d = din("kcT", (2, 128, 1024))
    vc_d = din("vc", (2, 128, 512))
    cs_d = din("cs", (128, 2 * NTS))
    mask_d = din("masks", (128, 1024))
    ident_d = din("ident", (128, 128))

    dbg_o = dout("dbg", (128, 2048)) if dbg else None
    es = ExitStack()
    with es:
        def sb(name, shape, dt):
            return es.enter_context(nc.sbuf_tensor(name, list(shape), dt))

        x = sb("x", (128, 8, NTS), F32)
        XB = [Buf(f"x{i}") for i in range(NTS // 128)]
        scr = sb("scr", (128, SCRW if not dbg else SCRW - 2200), F32)
        h = sb("h", (128, 8, 512), BF16)
        HB = Buf("h")
        wa = sb("wa", (128, NWA, 8, 256), BF16)
        WAB = [Buf(f"wa{i}") for i in range(NWA)]
        wd = sb("wd", (128, NWD, NFC, 128), BF16)
        WDB = [Buf(f"wd{i}") for i in range(NWD)]
        ones = sb("ones", (128, 128), BF16)
        ONESB = Buf("ones")
        e0 = sb("e0", (128, 128), BF16)
        ident = sb("ident_s", (128, 128), BF16)
        bmodT = sb("bmodT_s", (128, 288), F32)
        normwT = sb("normwT_s", (128, 192), F32)
        condr = sb("condr", (128, 16), F32)
        condb = sb("condb", (128, 8, 2), BF16)
        modsb = sb("modsb", (128, DEPTH, 2, 72), F32)
        coefA = sb("coefA", (128, DEPTH, 2, 3, 8), F32)
        coefG = sb("coefG", (128, DEPTH, 2, 3, 8), F32)
        CONSTB = Buf("const")
        masks = sb("masks_s", (128, 1024), BF16)
        ps = es.enter_context(nc.psum_tensor("ps", [128, 8, 512], F32))
        PSB = [Buf(f"ps{i}") for i in range(8)]
        esem = {e: es.enter_context(nc.semaphore(f"se_{e}")) for e in ("pe", "act", "dve", "pool")}
        dsem = {q: [es.enter_context(nc.semaphore(f"sd_{q}{i}")) for i in range(KQ)] for q in ("sp", "pool")}

        st = {"bank": 0, "wa": 0, "wd": 0}

        def bank():
            i = st["bank"]
            st["bank"] = (i + 1) % 6
            return ps[:, i, :], PSB[i]

        def sbank():
            i = 6 + st.get("sbank", 0)
            st["sbank"] = (i - 6 + 1) % 2
            return ps[:, i, :], PSB[i]

        class Carver:
            def __init__(self):
                self.off = 0

            def f32(self, n, *dims):
                ap = scr[:, self.off:self.off + n]
                self.off += n
                assert self.off <= SCRW, self.off
                return ap

            def bf(self, n):
                assert n % 2 == 0
                ap = scr[:, self.off:self.off + n // 2].bitcast(BF16)
                self.off += n // 2
                assert self.off <= SCRW, self.off
                return ap

        def load_wa(src2d, col0, ncols):
            i = st["wa"]
            st["wa"] = (i + 1) % NWA
            dst = wa[:, i, :, 0:ncols]
            src = src2d.rearrange("(k p) n -> p k n", p=128)[:, :, col0:col0 + ncols]
            P.add("pool", lambda e: e.dma_start(out=dst, in_=src), (), (WAB[i],), dma=True, exempt=True)
            return wa[:, i], WAB[i]

        def load_wd(src2d, col0):
            i = st["wd"]
            st["wd"] = (i + 1) % NWD
            dst = wd[:, i, :, :]
            src = src2d.rearrange("(f p) n -> p f n", p=128)[:, :, col0:col0 + 128]
            P.add("pool", lambda e: e.dma_start(out=dst, in_=src), (), (WDB[i],), dma=True, exempt=True)
            return wd[:, i], WDB[i]

        P.add("dve", lambda e: e.memset(ones[:], 1.0), (), (ONESB,))
        P.add("dve", lambda e: e.memset(e0[:], 0.0), (), (ONESB,))
        P.add("dve", lambda e: e.memset(e0[0:1, :], 1.0), (), (ONESB,))
        P.add("sp", lambda e: e.dma_start(out=bmodT[:], in_=bmodT_d), (), (CONSTB,), dma=True)
        P.add("sp", lambda e: e.dma_start(out=normwT[:], in_=normwT_d), (), (CONSTB,), dma=True)
        P.add("sp", lambda e: e.dma_start(out=condr[:], in_=cond_d), (), (CONSTB,), dma=True)
        P.add("pool", lambda e: e.dma_start(out=masks[:], in_=mask_d), (), (CONSTB,), dma=True)
        P.add("pool", lambda e: e.dma_start(out=ident[:], in_=ident_d), (), (CONSTB,), dma=True)
        P.add("act", lambda e: e.activation(out=condb[:].rearrange("p k r -> p (k r)"), in_=condr[:], func=AF.Silu),
              (CONSTB,), (CONSTB,))
        for l in range(depth):
            mp, mpb = bank()
            mpv = mp[:, 0:144].rearrange("p (f r) -> p f r", r=2)
            for pc in range(36):
                wt, wb = load_wa(w_mod[l], pc * 256, 256)
                for fi in range(2):
                    fc = pc * 2 + fi
                    for k in range(8):
                        P.add("pe", (lambda o, a, b, k=k: lambda e: e.matmul(o, lhsT=a, rhs=b, start=(k == 0), stop=(k == 7)))(
                            mpv[:, fc, :], wt[:, k, fi * 128:(fi + 1) * 128], condb[:, k, :]),
                            (wb, CONSTB), (mpb,))
            for r in range(2):
                P.add("dve", (lambda o, a, b: lambda e: e.tensor_tensor(out=o, in0=a, in1=b, op=ALU.add))(
                    modsb[:, l, r, :], mpv[:, :, r], bmodT[:, l * 72:(l + 1) * 72]), (mpb, CONSTB), (CONSTB,))
                for s in range(3):
                    wgt = 1.0 if s == 1 else 0.5
                    P.add("dve", (lambda o, a, b: lambda e: e.scalar_tensor_tensor(out=o, in0=a, scalar=1.0, in1=b, op0=ALU.add, op1=ALU.mult))(
                        coefA[:, l, r, s, :], modsb[:, l, r, (3 * s + 1) * 8:(3 * s + 2) * 8],
                        normwT[:, l * 48 + 2 * s * 8: l * 48 + 2 * s * 8 + 8]), (CONSTB,), (CONSTB,))
                    P.add("dve", (lambda o, a, b, w: lambda e: e.scalar_tensor_tensor(out=o, in0=a, scalar=w, in1=b, op0=ALU.mult, op1=ALU.mult))(
                        coefG[:, l, r, s, :], modsb[:, l, r, (3 * s + 2) * 8:(3 * s + 3) * 8],
                        normwT[:, l * 48 + (2 * s + 1) * 8: l * 48 + (2 * s + 1) * 8 + 8], wgt), (CONSTB,), (CONSTB,))
        P.barrier()

        def xbufs(tok0, n):
            return [XB[b] for b in range(tok0 // 128, (tok0 + n + 127) // 128)]

        def rstd_from(ssb, ssp, n, rs_ap, rsB, ndim):
            P.add("act", lambda e: e.activation(out=rs_ap[:, 0:n], in_=ssp[:, 0:n], func=AF.Sqrt, bias=epsc[:, 0:1], scale=1.0 / ndim),
                  (ssb, CONSTB), (rsB,))
            P.add("dve", lambda e: e.reciprocal(out=rs_ap[:, 0:n], in_=rs_ap[:, 0:n]), (rsB,), (rsB,))

        prepre = {"k": None, "next": None}

        def pre(l, r, s, tok0, n, K, stage="ab"):
            if stage == "ab" and prepre.get("k") == (l, s, tok0, n):
                prepre["k"] = None
                return
            xb = xbufs(tok0, n)
            rs_, rsB_ = K.get("rs2", K["rs"]), K.get("rs2B", K["rsB"])
            sq_, sqB_ = K.get("sqp", K["sq"]), K.get("sqpB", K["sqB"])

            def square(c):
                q = c % 4
                P.add("act", (lambda o, a: lambda e: e.activation(out=o, in_=a, func=AF.Square))(
                    sq_[:, q * 512:q * 512 + n], x[:, c, tok0:tok0 + n]), xb, (sqB_[q],))

            if "a" in stage:
                for c in range(4):
                    square(c)
                if stage == "a":
                    return
            ssp, ssb = sbank()
            for c in range(8):
                q = c % 4
                if c >= 4:
                    square(c)
                P.add("pe", (lambda o, b, c=c: lambda e: e.matmul(o, lhsT=ones[:], rhs=b, start=(c == 0), stop=(c == 7)))(
                    ssp[:, 0:n], sq_[:, q * 512:q * 512 + n]), (sqB_[q], ONESB), (ssb,))
            rstd_from(ssb, ssp, n, rs_, rsB_, D)
            for c in range(8):
                q = c % 2
                P.add("dve", (lambda o, a, sc, b: lambda e: e.scalar_tensor_tensor(out=o, in0=a, scalar=sc, in1=b, op0=ALU.mult, op1=ALU.mult))(
                    K["tmp"][:, q * 512:q * 512 + n], x[:, c, tok0:tok0 + n], coefA[:, l, r, s, c:c + 1], rs_[:, 0:n]),
                    xb + [rsB_, CONSTB], (K["tmpB"][q],))
                P.add("act", (lambda o, a, b: lambda e: e.activation(out=o, in_=a, func=AF.Identity, bias=b, scale=1.0))(
                    h[:, c, 0:n], K["tmp"][:, q * 512:q * 512 + n], modsb[:, l, r, 3 * s * 8 + c: 3 * s * 8 + c + 1]),
                    (K["tmpB"][q], CONSTB), (HB,))

        def post(l, r, s, tok0, n, K):
            xb = xbufs(tok0, n)
            rstd_from(K["ssb"], K["ssp"], n, K["rs"], K["rsB"], D)
            for c in range(8):
                q = c % 2
                P.add("dve", (lambda o, a, sc, b: lambda e: e.scalar_tensor_tensor(out=o, in0=a, scalar=sc, in1=b, op0=ALU.mult, op1=ALU.mult))(
                    K["tmp"][:, q * 512:q * 512 + n], K["ys"][:, c * 512:c * 512 + n], coefG[:, l, r, s, c:c + 1], K["rs"][:, 0:n]),
                    (K["ysB"], K["rsB"], CONSTB), (K["tmpB"][q],))
                P.add("dve", (lambda o, a, b: lambda e: e.tensor_tensor(out=o, in0=a, in1=b, op=ALU.add))(
                    x[:, c, tok0:tok0 + n], x[:, c, tok0:tok0 + n], K["tmp"][:, q * 512:q * 512 + n]),
                    xb + [K["tmpB"][q]], xb)

        def evac_y(K, yp, ypb, d, n, pend):
            q = d % 4
            P.add("act", (lambda o, a: lambda e: e.activation(out=o, in_=a, func=AF.Copy))(
                K["ys"][:, d * 512:d * 512 + n], yp[:, 0:n]), (ypb,), (K["ysB"],))
            P.add("act", (lambda o, a: lambda e: e.activation(out=o, in_=a, func=AF.Square))(
                K["sq"][:, q * 512:q * 512 + n], yp[:, 0:n]), (ypb,), (K["sqB"][q],))
            pend.append((d, q))

        def flush_ss(K, n, pend):
            while pend:
                d, q = pend.pop(0)
                P.add("pe", (lambda o, b, d=d: lambda e: e.matmul(o, lhsT=ones[:], rhs=b, start=(d == 0), stop=(d == 7)))(
                    K["ssp"][:, 0:n], K["sq"][:, q * 512:q * 512 + n]), (K["sqB"][q], ONESB), (K["ssb"],))

        def outproj_post(l, r, s, wsrc, tok0, n, K):
            K["ssp"], K["ssb"] = sbank()
            pend = []
            for pc in range(4):
                wt, wb = load_wa(wsrc, pc * 256, 256)
                for di in range(2):
                    d = pc * 2 + di
                    yp, ypb = bank()
                    for k in range(8):
                        P.add("pe", (lambda o, a, b, k=k: lambda e: e.matmul(o, lhsT=a, rhs=b, start=(k == 0), stop=(k == 7)))(
                            yp[:, 0:n], wt[:, k, di * 128:(di + 1) * 128], h[:, k, 0:n]), (wb, HB), (ypb,))
                    flush_ss(K, n, pend)
                    evac_y(K, yp, ypb, d, n, pend)
            flush_ss(K, n, pend)
            post(l, r, s, tok0, n, K)

        epsc = sb("epsc", (128, 1), F32)
        P.add("dve", lambda e: e.memset(epsc[:], EPS), (), (CONSTB,))

        def ffn_carve():
            C = Carver()
            K = {}
            K["a"] = C.bf(NFC * 512)
            K["aB"] = [Buf(f"a{f}") for f in range(NFC)]
            K["ys"] = C.f32(8 * 512)
            K["ysB"] = Buf("ys")
            K["tmp"] = C.f32(2 * 512)
            K["tmpB"] = [Buf("tmp0"), Buf("tmp1")]
            K["sq"] = C.bf(4 * 512)
            K["sqB"] = [Buf(f"sq{i}") for i in range(4)]
            K["sg"] = C.f32(2 * 512)
            K["sgB"] = [Buf("sg0"), Buf("sg1")]
            K["rs"] = C.f32(512)
            K["rsB"] = Buf("rs")
            K["rs2"] = C.f32(512)
            K["rs2B"] = Buf("rs2")
            K["sqp"] = C.bf(4 * 512)
            K["sqpB"] = [Buf(f"sqp{i}") for i in range(4)]
            return K

        dbgt = sb("dbgt", (128, 2048), F32) if dbg else None
        DBGB = Buf("dbg")
        dstate = {"done": False}

        def ffn_gateup(l, j, r, tok0, n, K):
            for pc in range(11):
                wg, wgb = load_wa(w_gate[l, j], pc * 256, 256)
                wu, wub = load_wa(w_up[l, j], pc * 256, 256)
                for fi in range(2):
                    f = pc * 2 + fi
                    gp, gpb = bank()
                    up, upb = bank()
                    for k in range(8):
                        P.add("pe", (lambda o, a, b, k=k: lambda e: e.matmul(o, lhsT=a, rhs=b, start=(k == 0), stop=(k == 7)))(
                            gp[:, 0:n], wg[:, k, fi * 128:(fi + 1) * 128], h[:, k, 0:n]), (wgb, HB), (gpb,))
                    for k in range(8):
                        P.add("pe", (lambda o, a, b, k=k: lambda e: e.matmul(o, lhsT=a, rhs=b, start=(k == 0), stop=(k == 7)))(
                            up[:, 0:n], wu[:, k, fi * 128:(fi + 1) * 128], h[:, k, 0:n]), (wub, HB), (upb,))
                    q = f % 2
                    P.add("act", (lambda o, a: lambda e: e.activation(out=o, in_=a, func=AF.Silu))(
                        K["sg"][:, q * 512:q * 512 + n], gp[:, 0:n]), (gpb,), (K["sgB"][q],))
                    P.add("dve", (lambda o, a, b: lambda e: e.tensor_tensor(out=o, in0=a, in1=b, op=ALU.mult))(
                        K["a"][:, f * 512:f * 512 + n], up[:, 0:n], K["sg"][:, q * 512:q * 512 + n]),
                        (upb, K["sgB"][q]), (K["aB"][f],))

        def ffn_down(l, j, r, tok0, n, K, mid_hook=None):
            s = 0 if j == 0 else 2
            K["ssp"], K["ssb"] = sbank()
            pend = []
            for d in range(8):
                wt, wb = load_wd(w_down[l, j], d * 128)
                yp, ypb = bank()
                for f in range(NFC):
                    P.add("pe", (lambda o, a, b, f=f: lambda e: e.matmul(o, lhsT=a, rhs=b, start=(f == 0), stop=(f == NFC - 1)))(
                        yp[:, 0:n], wt[:, f, :], K["a"][:, f * 512:f * 512 + n]), (wb, K["aB"][f]), (ypb,))
                flush_ss(K, n, pend)
                evac_y(K, yp, ypb, d, n, pend)
                if d == 1 and mid_hook is not None:
                    mid_hook()
            flush_ss(K, n, pend)
            post(l, r, s, tok0, n, K)

        def ffn_sublayer(l, j, r, tiles, K):
            s = 0 if j == 0 else 2
            pre(l, r, s, tiles[0][0], tiles[0][1], K)
            for ti, (t0, n) in enumerate(tiles):
                ffn_gateup(l, j, r, t0, n, K)
                hook = None
                if ti + 1 < len(tiles):
                    nt0, nn = tiles[ti + 1]
                    pre(l, r, s, nt0, nn, K, stage="a")
                    hook = (lambda nt0=nt0, nn=nn: pre(l, r, s, nt0, nn, K, stage="b"))
                elif prepre["next"] is not None and len(tiles) > 1:
                    (l2, s2, n2) = prepre["next"]
                    pre(l2, r, s2, 0, n2, K, stage="a")

                    def hook(l2=l2, s2=s2, n2=n2):
                        pre(l2, r, s2, 0, n2, K, stage="b")
                        prepre["k"] = (l2, s2, 0, n2)
                ffn_down(l, j, r, t0, n, K, mid_hook=hook)

        def A_(eng, fn, R=(), W=(), dma=False):
            return P.add(eng, fn, tuple(R), tuple(W), dma=dma)

        def mmk(o, a, b, first, last, R, W):
            A_("pe", lambda e: e.matmul(o, lhsT=a, rhs=b, start=first, stop=last), R, W)

        def proj_fm(wt, wb, c0, n, out_ps, out_b):
            for k in range(8):
                mmk(out_ps[:, 0:n], wt[:, k, c0:c0 + 128], h[:, k, 0:n], k == 0, k == 7, (wb, HB), (out_b,))

        def attn_layer(l, ph, r, NT, flush=True):
            j = l // 2
            NB = NT // 128
            C = Carver()
            K = {}
            kT = C.bf(4 * 1024); KB = [Buf(f"k{i}") for i in range(8)]
            vt = C.bf(8 * 256); VB = [Buf(f"v{i}") for i in range(8)]
            qT = C.bf(8 * 768); QB = [Buf(f"q{i}") for i in range(6)]
            cs = C.bf(2 * NTS); CSB = Buf("cs")
            NPT = 5
            pT = C.bf(NPT * 512); PTB = [Buf(f"pt{i}") for i in range(NPT)]
            K["sq"] = C.bf(4 * 512); K["sqB"] = [Buf(f"sq{i}") for i in range(4)]
            K["tmp"] = C.f32(2 * 512); K["tmpB"] = [Buf("tmp0"), Buf("tmp1")]
            K["rs"] = C.f32(512); K["rsB"] = Buf("rs")
            K["ys"] = C.f32(8 * 512); K["ysB"] = Buf("ys")
            kc = C.bf(4 * 256); vcx = C.bf(2 * 256); esk = C.bf(2048); LB = Buf("lconst")
            rden = C.f32(512); RDB = Buf("rden")
            ptc = {"i": 0}
            A_("pool", lambda e: e.dma_start(out=kc, in_=kcT_d[j]), (), (LB,), dma=True)
            A_("pool", lambda e: e.dma_start(out=vcx, in_=vc_d[j]), (), (LB,), dma=True)
            A_("pool", lambda e: e.dma_start(out=esk, in_=sink_d[j]), (), (LB,), dma=True)
            A_("act", lambda e: e.activation(out=esk, in_=esk, func=AF.Exp), (LB,), (LB,))
            if ph == "S":
                A_("pool", lambda e: e.dma_start(out=cs, in_=cs_d), (), (CSB,), dma=True)

            def rope_or_copy(src_ps, srcb, perm_ps, permb, tok0, n, dsts):
                if ph == "P":
                    for (dst, db, c0, ncol) in dsts:
                        A_("act", (lambda o, a: lambda e: e.activation(out=o, in_=a, func=AF.Copy))(dst, src_ps[:, c0:c0 + ncol]), (srcb,), (db,))
                    return
                t1 = K["tmp"][:, 0:n]
                t2 = K["tmp"][:, 512:512 + n]
                A_("dve", lambda e: e.tensor_tensor(out=t1, in0=src_ps[:, 0:n], in1=cs[:, tok0:tok0 + n], op=ALU.mult), (srcb, CSB), (K["tmpB"][0],))
                A_("dve", lambda e: e.tensor_tensor(out=t2, in0=perm_ps[:, 0:n], in1=cs[:, NTS + tok0:NTS + tok0 + n], op=ALU.mult), (permb, CSB), (K["tmpB"][1],))
                for (dst, db, c0, ncol) in dsts:
                    A_("dve", (lambda o, a, b: lambda e: e.tensor_tensor(out=o, in0=a, in1=b, op=ALU.add))(
                        dst, K["tmp"][:, c0:c0 + ncol], K["tmp"][:, 512 + c0:512 + c0 + ncol]), K["tmpB"], (db,))

            STAGE = int(os.environ.get("ATT_STAGE", "99"))

            def project(tok0, n):
                b0 = tok0 // 128
                nb = n // 128
                if STAGE < 1:
                    return
                pre(l, r, 1, tok0, n, K)
                for pc in range(4 if STAGE >= 2 else 0):
                    wq, wqb = load_wa(w_att[j], pc * 256, 256)
                    if ph == "S":
                        wp, wpb = load_wa(w_att[j], 1024 + pc * 256, 256)
                    for ci in range(2):
                        c = pc * 2 + ci
                        qp, qpb = bank()
                        proj_fm(wq, wqb, ci * 128, n, qp, qpb)
                        pp, ppb = (None, None)
                        if ph == "S":
                            pp, ppb = bank()
                            proj_fm(wp, wpb, ci * 128, n, pp, ppb)
                        dsts = [(qT[:, c * 768 + ((b0 + bi) % 6) * 128: c * 768 + ((b0 + bi) % 6) * 128 + 128], QB[(b0 + bi) % 6], bi * 128, 128)
                                for bi in range(nb)]
                        rope_or_copy(qp, qpb, pp, ppb, tok0, n, dsts)
                ks0 = (b0 % 8)
                for pc in range(2 if STAGE >= 3 else 0):
                    wk, wkb = load_wa(w_att[j], 2048 + pc * 256, 256)
                    if ph == "S":
                        wp, wpb = load_wa(w_att[j], 2560 + pc * 256, 256)
                    for ci in range(2):
                        kv = pc * 2 + ci
                        kp, kpb = bank()
                        proj_fm(wk, wkb, ci * 128, n, kp, kpb)
                        pp, ppb = (None, None)
                        if ph == "S":
                            pp, ppb = bank()
                            proj_fm(wp, wpb, ci * 128, n, pp, ppb)
                        dsts = [(kT[:, kv * 1024 + ks0 * 128: kv * 1024 + ks0 * 128 + n], None, 0, n)]
                        if ph == "P":
                            A_("act", (lambda o, a: lambda e: e.activation(out=o, in_=a, func=AF.Copy))(dsts[0][0], kp[:, 0:n]), (kpb,), KB[ks0:ks0 + nb])
                        else:
                            t1 = K["tmp"][:, 0:n]
                            t2 = K["tmp"][:, 512:512 + n]
                            A_("dve", (lambda a: lambda e: e.tensor_tensor(out=t1, in0=a, in1=cs[:, tok0:tok0 + n], op=ALU.mult))(kp[:, 0:n]), (kpb, CSB), (K["tmpB"][0],))
                            A_("dve", (lambda a: lambda e: e.tensor_tensor(out=t2, in0=a, in1=cs[:, NTS + tok0:NTS + tok0 + n], op=ALU.mult))(pp[:, 0:n]), (ppb, CSB), (K["tmpB"][1],))
                            A_("dve", (lambda o: lambda e: e.tensor_tensor(out=o, in0=t1, in1=t2, op=ALU.add))(dsts[0][0]), K["tmpB"], KB[ks0:ks0 + nb])
                if STAGE < 4:
                    return
                if ph == "P":
                    wkk, wkkb = load_wa(w_att[j], 3072, 256)
                wvv, wvvb = load_wa(w_att[j], 3328, 256)
                for bi in range(nb):
                    blk = b0 + bi
                    vp, vpb = bank()
                    if ph == "P":
                        vp2, vpb2 = bank()
                        for k in range(8):
                            mmk(vp[:, 0:256], h[:, k, bi * 128:(bi + 1) * 128], wkk[:, k, :], k == 0, k == 7, (HB, wkkb), (vpb,))
                        for k in range(8):
                            mmk(vp2[:, 0:256], h[:, k, bi * 128:(bi + 1) * 128], wvv[:, k, :], k == 0, k == 7, (HB, wvvb), (vpb2,))
                        kvo = K["ys"][:, (bi % 2) * 512:(bi % 2) * 512 + 512]
                        A_("act", (lambda o, a: lambda e: e.activation(out=o, in_=a, func=AF.Copy))(kvo[:, 0:256], vp[:, 0:256]), (vpb,), (K["ysB"],))
                        A_("act", (lambda o, a: lambda e: e.activation(out=o, in_=a, func=AF.Copy))(kvo[:, 256:512], vp2[:, 0:256]), (vpb2,), (K["ysB"],))
                        if STAGE >= 5:
                            A_("sp", (lambda o, a: lambda e: e.dma_start(out=o, in_=a))(nk_o[j, blk // 2, (blk % 2) * 128:(blk % 2) * 128 + 128, :], kvo[:, 0:256]), (K["ysB"],), (), dma=True)
                            A_("sp", (lambda o, a: lambda e: e.dma_start(out=o, in_=a))(nv_o[j, blk // 2, (blk % 2) * 128:(blk % 2) * 128 + 128, :], kvo[:, 256:512]), (K["ysB"],), (), dma=True)
                        A_("dve", (lambda o, a: lambda e: e.tensor_copy(out=o, in_=a))(vt[:, (blk % 8) * 256:(blk % 8) * 256 + 256], kvo[:, 256:512]), (K["ysB"],), (VB[blk % 8],))
                    else:
                        for k in range(8):
                            mmk(vp[:, 0:256], h[:, k, bi * 128:(bi + 1) * 128], wvv[:, k, :], k == 0, k == 7, (HB, wvvb), (vpb,))
                        A_("act", (lambda o, a: lambda e: e.activation(out=o, in_=a, func=AF.Copy))(vt[:, (blk % 8) * 256:(blk % 8) * 256 + 256], vp[:, 0:256]), (vpb,), (VB[blk % 8],))

            AST = int(os.environ.get("ATT_A", "99"))

            def attend(i, keys, gi):
                qs = i % 6
                for kv in range(4):
                    pts = []
                    for (kind, kb, msk) in keys:
                        spA, spAb = bank()
                        spB, spBb = bank()
                        for hh in range(4):
                            c = 2 * kv + hh // 2
                            lo = (hh % 2) * 64
                            sp_, spb = (spA, spAb) if hh % 2 == 0 else (spB, spBb)
                            if kind == "l":
                                ksl = kT[lo:lo + 64, kv * 1024 + (kb % 8) * 128: kv * 1024 + (kb % 8) * 128 + 128]
                                kbuf = KB[kb % 8]
                            else:
                                ksl = kc[lo:lo + 64, kv * 256 + kb * 128: kv * 256 + kb * 128 + 128]
                                kbuf = LB
                            mmk(sp_[:, (hh // 2) * 128:(hh // 2 + 1) * 128], ksl, qT[lo:lo + 64, c * 768 + qs * 128: c * 768 + qs * 128 + 128], True, True,
                                (kbuf, QB[qs]), (spb,))
                        if AST < 2:
                            continue
                        pi = ptc["i"]
                        ptc["i"] = (pi + 1) % NPT
                        pt = pT[:, pi * 512:(pi + 1) * 512]
                        A_("act", (lambda o, a: lambda e: e.activation(out=o, in_=a, func=AF.Exp, scale=0.125))(pt[:, 0:256], spA[:, 0:256]), (spAb,), (PTB[pi],))
                        A_("act", (lambda o, a: lambda e: e.activation(out=o, in_=a, func=AF.Exp, scale=0.125))(pt[:, 256:512], spB[:, 0:256]), (spBb,), (PTB[pi],))
                        if msk:
                            mk = masks[:, (msk - 1) * 512: msk * 512]
                            A_("dve", (lambda o, m: lambda e: e.tensor_tensor(out=o, in0=o, in1=m, op=ALU.mult))(pt, mk), (PTB[pi], CONSTB), (PTB[pi],))
                        pts.append((pt, PTB[pi], kind, kb))
                    if AST < 3:
                        continue
                    rsp, rsb = sbank()
                    for n_, (pt, pb, kind, kb) in enumerate(pts):
                        mmk(rsp[:, :], ones[:, :], pt, n_ == 0, False, (pb, ONESB), (rsb,))
                    mmk(rsp[:, :], e0[:, :], esk[:, kv * 512:(kv + 1) * 512], False, True, (LB, ONESB), (rsb,))
                    if AST < 4:
                        continue
                    obanks = [bank(), bank()]
                    for hh in range(4):
                        c2 = hh // 2
                        op_, opb = obanks[c2]
                        lo = (hh % 2) * 64
                        for n_, (pt, pb, kind, kb) in enumerate(pts):
                            if kind == "l":
                                vsl = vt[:, (kb % 8) * 256 + kv * 64:(kb % 8) * 256 + kv * 64 + 64]
                                vbuf = VB[kb % 8]
                            else:
                                vsl = vcx[:, kb * 256 + kv * 64: kb * 256 + kv * 64 + 64]
                                vbuf = LB
                            mmk(op_[lo:lo + 64, 0:128], vsl, pt[:, (hh % 2) * 256 + (hh // 2) * 128:(hh % 2) * 256 + (hh // 2) * 128 + 128], n_ == 0, n_ == len(pts) - 1,
                                (pb, vbuf), (opb,))
                    if AST < 5:
                        continue
                    A_("dve", (lambda a: lambda e: e.reciprocal(out=rden[:, :], in_=a))(rsp[:, :]), (rsb,), (RDB,))
                    for hh in range(4):
                        c2 = hh // 2
                        op_, opb = obanks[c2]
                        lo = (hh % 2) * 64
                        A_("dve", (lambda o, a, b: lambda e: e.tensor_tensor(out=o, in0=a, in1=b, op=ALU.mult))(
                            h[lo:lo + 64, 2 * kv + c2, gi * 128:(gi + 1) * 128], op_[lo:lo + 64, 0:128], rden[lo:lo + 64, (hh % 2) * 256 + (hh // 2) * 128:(hh % 2) * 256 + (hh // 2) * 128 + 128]),
                            (opb, RDB), (HB,))

            def group(blocks):
                for gi, i in enumerate(blocks):
                    if ph == "P":
                        s0 = (i // 2) * 2
                        keys = [("l", s0, 0), ("l", s0 + 1, 0)]
                    else:
                        keys = []
                        if i - 1 >= 0:
                            keys.append(("l", i - 1, 1))
                        keys.append(("l", i, 0))
                        if i + 1 < NB:
                            keys.append(("l", i + 1, 2))
                        keys += [("c", 0, 0), ("c", 1, 0)]
                    if "att" in mix:
                        attend(i, keys, gi)
                if "op" in mix:
                    outproj_post(l, r, 1, w_o[j], blocks[0] * 128, len(blocks) * 128, K)

            for t0 in range(0, NT, 512):
                n = min(512, NT - t0)
                project(t0, n)
                b0 = t0 // 128
                if ph == "P":
                    group([b0, b0 + 1, b0 + 2, b0 + 3])
                else:
                    group([b for b in range(b0 - 1, b0 + n // 128 - 1) if b >= 0])
            if ph == "S" and flush:
                group([NB - 1])


        def gelu_tanh(srcs, dst, dstb, n, K, accum=None):
            xs = K["gl"][:, 0:n]
            tt = K["gl"][:, 512:512 + n]
            for (sp_ap, sbuf_, c0, ncol) in srcs:
                A_("act", (lambda o, a: lambda e: e.activation(out=o, in_=a, func=AF.Copy))(K["gl"][:, c0:c0 + ncol], sp_ap), (sbuf_,), (K["glB"][0],))
                A_("act", (lambda o, a: lambda e: e.activation(out=o, in_=a, func=AF.Square))(K["gl"][:, 512 + c0:512 + c0 + ncol], sp_ap), (sbuf_,), (K["glB"][1],))
            A_("dve", lambda e: e.tensor_scalar(out=tt, in0=tt, scalar1=0.044715, scalar2=1.0, op0=ALU.mult, op1=ALU.add), (K["glB"][1],), (K["glB"][1],))
            A_("dve", lambda e: e.tensor_tensor(out=tt, in0=tt, in1=xs, op=ALU.mult), K["glB"], (K["glB"][1],))
            A_("act", lambda e: e.activation(out=tt, in_=tt, func=AF.Sigmoid, scale=1.5957691216057308), (K["glB"][1],), (K["glB"][1],))
            if accum is None:
                A_("dve", lambda e: e.tensor_tensor(out=dst, in0=tt, in1=xs, op=ALU.mult), K["glB"], (dstb,))
            else:
                A_("dve", lambda e: e.scalar_tensor_tensor(out=dst, in0=tt, scalar=1.0, in1=xs, op0=ALU.mult, op1=ALU.mult, accum_out=accum), K["glB"], (dstb,))

        def conv_layer(l, ph, r, NT, flush=True):
            j = l // 2
            NB = NT // 128
            C = Carver()
            K = {}
            GW = 672
            G = C.bf(4 * GW); GB = Buf("G")
            NDG = 16
            dg = C.bf(NDG * 128); DGB = [Buf(f"dg{i}") for i in range(NDG)]
            dgc = {"i": 0}
            BO = C.bf(4 * 640); BOB = Buf("BO")
            vtg = C.bf(4 * 512); VGB = [Buf(f"vg{i}") for i in range(4)]
            sgm = C.f32(2 * 512); SGB = [Buf("sgm0"), Buf("sgm1")]
            K["gl"] = C.f32(2 * 512); K["glB"] = [Buf("gl0"), Buf("gl1")]
            cgb = C.bf(4 * 512); CGB = Buf("cg")
            mu = C.f32(512); var = C.f32(512); rstd = C.f32(512); STB = Buf("stat")
            K["ys"] = C.f32(8 * 512); K["ysB"] = Buf("ys")
            K["sq"] = C.bf(4 * 512); K["sqB"] = [Buf(f"sq{i}") for i in range(4)]
            cgq = K["sq"]
            K["tmp"] = C.f32(2 * 512); K["tmpB"] = [Buf("tmp0"), Buf("tmp1")]
            K["rs"] = C.f32(512); K["rsB"] = Buf("rs")
            slnv = C.f32(1024); convw = C.f32(124); convv = C.f32(12); sguw = C.bf(512); sgub = C.bf(2048); LB = Buf("lconst")
            sts = C.f32(8); SSB = Buf("sts")
            A_("sp", lambda e: e.dma_start(out=slnv, in_=slnv_d[j]), (), (LB,), dma=True)
            A_("sp", lambda e: e.dma_start(out=convw, in_=convw_d[ph][j]), (), (LB,), dma=True)
            A_("sp", lambda e: e.dma_start(out=convv, in_=convv_d[j]), (), (LB,), dma=True)
            A_("pool", lambda e: e.dma_start(out=sguw, in_=sguw_d[ph][j]), (), (LB,), dma=True)
            A_("pool", lambda e: e.dma_start(out=sgub, in_=sgub_d[ph][j]), (), (LB,), dma=True)
            A_("dve", lambda e: e.memset(G, 0.0), (), (GB,))

            def inproj(tok0, n, gsegs, bo0):
                nb = n // 128
                pre(l, r, 1, tok0, n, K)
                for pc in range(2):
                    wv_, wvb = load_wa(w_in[j], pc * 256, 256)
                    wg_, wgb = load_wa(w_in[j], 512 + pc * 256, 256)
                    for ci in range(2):
                        cc = pc * 2 + ci
                        avp, avb = bank()
                        agp, agb = bank()
                        proj_fm(wv_, wvb, ci * 128, n, avp, avb)
                        proj_fm(wg_, wgb, ci * 128, n, agp, agb)
                        q = cc % 2
                        A_("act", (lambda o, a: lambda e: e.activation(out=o, in_=a, func=AF.Sigmoid))(sgm[:, q * 512:q * 512 + n], agp[:, 0:n]), (agb,), (SGB[q],))
                        for (c0, ncol, gu0) in gsegs:
                            A_("dve", (lambda o, a, b: lambda e: e.tensor_tensor(out=o, in0=a, in1=b, op=ALU.mult))(
                                G[:, cc * GW + gu0: cc * GW + gu0 + ncol], avp[:, c0:c0 + ncol], sgm[:, q * 512 + c0:q * 512 + c0 + ncol]), (avb, SGB[q]), (GB,))
                wv0, wv0b = load_wa(w_in[j], 1536, 256)
                wv1, wv1b = load_wa(w_in[j], 1792, 256)
                for bi in range(nb):
                    vp, vpb = bank()
                    vp2, vpb2 = bank()
                    for k in range(8):
                        mmk(vp[:, 0:256], h[:, k, bi * 128:(bi + 1) * 128], wv0[:, k, :], k == 0, k == 7, (HB, wv0b), (vpb,))
                    for k in range(8):
                        mmk(vp2[:, 0:256], h[:, k, bi * 128:(bi + 1) * 128], wv1[:, k, :], k == 0, k == 7, (HB, wv1b), (vpb2,))
                    gq = sgm[:, (bi % 2) * 512:(bi % 2) * 512 + 512]
                    gqb = SGB[bi % 2]
                    gelu_tanh([(vp[:, 0:256], vpb, 0, 256), (vp2[:, 0:256], vpb2, 256, 256)], gq, gqb, 512, K, accum=sts[:, 0:1])
                    A_("dve", lambda e: e.tensor_scalar(out=sts[:, 1:2], in0=sts[:, 0:1], scalar1=-1.0 / 512, scalar2=None, op0=ALU.mult), (gqb,), (SSB,))
                    A_("act", (lambda a: lambda e: e.activation(out=K["gl"][:, 0:512], in_=a, func=AF.Square, bias=sts[:, 1:2], scale=1.0, accum_out=sts[:, 2:3]))(gq), (gqb, SSB), (K["glB"][0], SSB))
                    A_("act", lambda e: e.activation(out=sts[:, 3:4], in_=sts[:, 2:3], func=AF.Sqrt, bias=epsc[:, 0:1], scale=1.0 / 512), (SSB, CONSTB), (SSB,))
                    A_("dve", lambda e: e.reciprocal(out=sts[:, 3:4], in_=sts[:, 3:4]), (SSB,), (SSB,))
                    A_("dve", (lambda a: lambda e: e.tensor_scalar(out=a, in0=a, scalar1=sts[:, 1:2], scalar2=sts[:, 3:4], op0=ALU.add, op1=ALU.mult))(gq), (gqb, SSB), (gqb,))
                    A_("dve", (lambda a: lambda e: e.tensor_tensor(out=a, in0=a, in1=slnv[:, 0:512], op=ALU.mult))(gq), (gqb, LB), (gqb,))
                    A_("dve", (lambda o, a: lambda e: e.tensor_tensor(out=o, in0=a, in1=slnv[:, 512:1024], op=ALU.add))(vtg[:, bi * 512:(bi + 1) * 512], gq), (gqb, LB), (VGB[bi],))
                for pc in range(2):
                    wu_, wub = load_wa(w_in[j], 1024 + pc * 256, 256)
                    for ci in range(2):
                        g = pc * 2 + ci
                        up, upb = bank()
                        proj_fm(wu_, wub, ci * 128, n, up, upb)
                        ug = sgm[:, (g % 2) * 512:(g % 2) * 512 + n]
                        ugb = SGB[g % 2]
                        gelu_tanh([(up[:, 0:n], upb, 0, n)], ug, ugb, n, K)
                        spp, sppb = bank()
                        mmk(spp[:, 0:n], e0[:, :], sgub[:, g * 512: g * 512 + n], True, False, (ONESB, LB), (sppb,))
                        for bi in range(nb):
                            mmk(spp[:, bi * 128:(bi + 1) * 128], vtg[:, bi * 512 + g * 128: bi * 512 + g * 128 + 128], sguw[:, g * 128:(g + 1) * 128], False, bi == nb - 1, (VGB[bi], LB), (sppb,))
                        A_("dve", (lambda o, a, b: lambda e: e.tensor_tensor(out=o, in0=a, in1=b, op=ALU.mult))(
                            BO[:, g * 640 + bo0: g * 640 + bo0 + n], spp[:, 0:n], ug), (sppb, ugb), (BOB,))

            def group(tokg0, n, csegs, bo0):
                cps = [bank() for cc in range(4)]
                assert len(csegs) == 1
                (c0, ncol, gu0) = csegs[0]
                for cc in range(4):
                    cp, cpb = cps[cc]
                    for k in range(31):
                        di = dgc["i"]
                        dgc["i"] = (di + 1) % NDG
                        dsl = dg[:, di * 128:(di + 1) * 128]
                        A_("pool", (lambda o, w: lambda e: e.tensor_scalar(out=o, in0=ident[:, :], scalar1=w, scalar2=0.0, op0=ALU.mult, op1=ALU.add))(
                            dsl, convw[:, cc * 31 + k: cc * 31 + k + 1]), (LB, CONSTB), (DGB[di],))
                        src = G[:, cc * GW + gu0 - 15 + k: cc * GW + gu0 - 15 + k + ncol]
                        mmk(cp[:, c0:c0 + ncol], dsl, src, k == 0, k == 30, (DGB[di], GB), (cpb,))
                for cc in range(4):
                    cp, cpb = cps[cc]
                    A_("act", (lambda o, a, b: lambda e: e.activation(out=o, in_=a, func=AF.Identity, bias=b, scale=1.0))(cgb[:, cc * 512:cc * 512 + n], cp[:, 0:n], convv[:, cc:cc + 1]), (cpb, LB), (CGB,))
                    A_("act", (lambda o, a, b: lambda e: e.activation(out=o, in_=a, func=AF.Square, bias=b, scale=1.0))(cgq[:, cc * 512:cc * 512 + n], cp[:, 0:n], convv[:, cc:cc + 1]), (cpb, LB), (K["sqB"][cc],))
                s1, s1b = sbank()
                for cc in range(4):
                    mmk(s1[:, 0:n], ones[:, :], cgb[:, cc * 512:cc * 512 + n], cc == 0, cc == 3, (CGB, ONESB), (s1b,))
                s2, s2b = sbank()
                for cc in range(4):
                    mmk(s2[:, 0:n], ones[:, :], cgq[:, cc * 512:cc * 512 + n], cc == 0, cc == 3, (K["sqB"][cc], ONESB), (s2b,))
                A_("dve", lambda e: e.tensor_scalar(out=mu[:, 0:n], in0=s1[:, 0:n], scalar1=1.0 / 512, scalar2=None, op0=ALU.mult), (s1b,), (STB,))
                A_("dve", lambda e: e.tensor_tensor(out=var[:, 0:n], in0=mu[:, 0:n], in1=mu[:, 0:n], op=ALU.mult), (STB,), (STB,))
                A_("dve", lambda e: e.scalar_tensor_tensor(out=var[:, 0:n], in0=s2[:, 0:n], scalar=1.0 / 512, in1=var[:, 0:n], op0=ALU.mult, op1=ALU.subtract), (s2b, STB), (STB,))
                A_("act", lambda e: e.activation(out=rstd[:, 0:n], in_=var[:, 0:n], func=AF.Sqrt, bias=epsc[:, 0:1], scale=1.0), (STB, CONSTB), (STB,))
                A_("dve", lambda e: e.reciprocal(out=rstd[:, 0:n], in_=rstd[:, 0:n]), (STB,), (STB,))
                for cc in range(4):
                    cp, cpb = cps[cc]
                    q = cc % 2
                    lt = K["tmp"][:, q * 512:q * 512 + n]
                    A_("dve", (lambda o, a, b: lambda e: e.scalar_tensor_tensor(out=o, in0=a, scalar=b, in1=mu[:, 0:n], op0=ALU.add, op1=ALU.subtract))(lt, cp[:, 0:n], convv[:, cc:cc + 1]), (cpb, LB, STB), (K["tmpB"][q],))
                    A_("dve", (lambda o: lambda e: e.tensor_tensor(out=o, in0=o, in1=rstd[:, 0:n], op=ALU.mult))(lt), (K["tmpB"][q], STB), (K["tmpB"][q],))
                    A_("act", (lambda o, a, sc, b: lambda e: e.activation(out=o, in_=a, func=AF.Silu, bias=b, scale=sc))(h[:, cc, 0:n], lt, convv[:, 4 + cc:5 + cc], convv[:, 8 + cc:9 + cc]), (K["tmpB"][q], LB), (HB,))
                    A_("act", (lambda o, a: lambda e: e.activation(out=o, in_=a, func=AF.Copy))(h[:, 4 + cc, 0:n], BO[:, cc * 640 + bo0: cc * 640 + bo0 + n]), (BOB,), (HB,))
                outproj_post(l, r, 1, w_out[j], tokg0, n, K)

            if "grp" not in mix:
                group = lambda *a: None
            if ph == "P":
                for t0 in range(0, NT, 512):
                    inproj(t0, 512, [(0, 256, 16), (256, 256, 304)], 0)
                    group(t0, 256, [(0, 256, 16)], 0)
                    group(t0 + 256, 256, [(0, 256, 304)], 256)
            else:
                for t0 in range(0, NT, 512):
                    n = min(512, NT - t0)
                    if t0 > 0:
                        for cc in range(4):
                            A_("act", (lambda o, a: lambda e: e.activation(out=o, in_=a, func=AF.Copy))(G[:, cc * GW: cc * GW + 144], G[:, cc * GW + 512: cc * GW + 656]), (GB,), (GB,))
                            A_("act", (lambda o, a: lambda e: e.activation(out=o, in_=a, func=AF.Copy))(BO[:, cc * 640: cc * 640 + 128], BO[:, cc * 640 + 512: cc * 640 + 640]), (BOB,), (BOB,))
                    inproj(t0, n, [(0, n, 144)], 128)
                    if t0 == 0:
                        group(0, n - 128, [(0, n - 128, 144)], 128)
                    else:
                        group(t0 - 128, n, [(0, n, 16)], 0)
                if flush:
                    assert NT % 512 == 0
                    group(NT - 128, 128, [(0, 128, 528)], 512)

        for ph in phases:
            NT = NTP if ph == "P" else NTS
            r = 0 if ph == "P" else 1
            for c in range(8):
                P.add("sp", (lambda o, a: lambda e: e.dma_start(out=o, in_=a))(x[:, c, 0:NT], xT[ph][c * 128:(c + 1) * 128, :]),
                      (), XB[0:NT // 128], dma=True)
            def mk_tiles(ntok):
                return [(t0, min(512, ntok - t0)) for t0 in range(0, ntok, 512)]
            for l in range(depth):
                shrink = (ph == "S") and SHRINK and depth == DEPTH
                nt1 = NT - 128 * l if shrink else NT
                nt2 = nt1 - 128 if shrink else NT
                K = ffn_carve()
                prepre["next"] = (l, 1, min(512, nt1)) if (do_mixer and PREPRE) else None
                ffn_sublayer(l, 0, r, mk_tiles(nt1), K)
                P.barrier()
                if do_mixer:
                    if l % 2 == 1:
                        if "attn" in mix:
                            attn_layer(l, ph, r, nt1, flush=not shrink)
                    else:
                        if "conv" in mix:
                            conv_layer(l, ph, r, nt1, flush=not shrink)
                    P.barrier()
                K = ffn_carve()
                nxt1 = (NT - 128 * (l + 1)) if shrink else NT
                prepre["next"] = (l + 1, 0, min(512, nxt1)) if (l + 1 < depth and PREPRE) else None
                ffn_sublayer(l, 1, r, mk_tiles(nt2), K)
                P.barrier()
            NO = NTP if ph == "P" else NOWN
            for c in range(8):
                P.add("sp", (lambda o, a: lambda e: e.dma_start(out=o, in_=a))(yT[ph][c * 128:(c + 1) * 128, :], x[:, c, 0:NO]),
                      XB[0:NO // 128], (), dma=True)
            P.barrier()

        cnt = P.finalize()
        print("ops", len(P.ops), "signals", cnt, "dmas", P.ndma)
        with nc.Block() as block:
            @block.tensor
            def _(e):
                P.emit("pe", e, esem, dsem)

            @block.scalar
            def _(e):
                P.emit("act", e, esem, dsem)

            @block.vector
            def _(e):
                P.emit("dve", e, esem, dsem)

            @block.gpsimd
            def _(e):
                P.emit("pool", e, esem, dsem)

            @block.sync
            def _(e):
                P.emit("sp", e, esem, dsem)
    return nc


_PERM = np.concatenate([np.arange(16, 32), np.arange(0, 16), np.arange(48, 64), np.arange(32, 48)])


def _fm(v, nch):
    return np.ascontiguousarray(np.asarray(v, np.float32).reshape(nch, 128).T)


def _prep_shared(inp):
    sh = {}
    for k in ("w_mod", "ffn_w_gate", "ffn_w_up", "ffn_w_down", "cm_w_in", "cm_w_out", "attn_w_o"):
        sh[k] = np.ascontiguousarray(inp[k], dtype=np.float32)
    wq = np.asarray(inp["attn_w_qkv"], np.float32)
    blocks = []
    for j in range(2):
        q = wq[j][:, :1024]
        k = wq[j][:, 1024:1280]
        qperm = q.reshape(1024, 16, 64)[:, :, _PERM].reshape(1024, 1024)
        k4 = k.reshape(1024, 4, 64)
        kdup = np.concatenate([k4, k4], axis=2).reshape(1024, 512)
        kp4 = k4[:, :, _PERM]
        kpdup = np.concatenate([kp4, kp4], axis=2).reshape(1024, 512)
        blocks.append(np.concatenate([q, qperm, kdup, kpdup, wq[j][:, 1024:1536]], axis=1))
    sh["w_att"] = np.ascontiguousarray(np.stack(blocks))
    bm = np.asarray(inp["b_mod"], np.float32)
    sh["bmodT"] = np.ascontiguousarray(np.concatenate([_fm(bm[l], 72) for l in range(DEPTH)], axis=1))
    nw = np.asarray(inp["norm_w"], np.float32)
    sh["normwT"] = np.ascontiguousarray(np.concatenate([_fm(nw[l].reshape(-1), 48) for l in range(DEPTH)], axis=1))
    cv = []
    sl = []
    for j in range(2):
        cv.append(np.concatenate([_fm(inp[k][j], 4) for k in ("cm_conv_b", "cm_conv_ln_g", "cm_conv_ln_b")], axis=1))
        sl.append(np.concatenate([np.tile(np.asarray(inp[k][j], np.float32)[None, :], (128, 1))
                                  for k in ("cm_sgu_ln_g", "cm_sgu_ln_b")], axis=1))
    sh["convv"] = np.ascontiguousarray(np.stack(cv))
    sh["slnv"] = np.ascontiguousarray(np.stack(sl))
    sk = np.asarray(inp["attn_sink"], np.float32)
    sk = sk.reshape(2, 4, 4)[:, :, [0, 2, 1, 3]].reshape(2, 16)
    sh["sinkrow"] = np.ascontiguousarray(np.tile(np.repeat(sk, 128, axis=1).reshape(2, 1, 2048), (1, 128, 1)))
    a = np.arange(128)
    mprev = (a[None, :] <= a[:, None]).astype(np.float32)
    mnext = (a[:, None] <= a[None, :]).astype(np.float32)
    sh["ident"] = np.ascontiguousarray(np.eye(128, dtype=np.float32))
    sh["masks"] = np.ascontiguousarray(np.concatenate([np.tile(mprev, (1, 4)), np.tile(mnext, (1, 4))], axis=1))
    return sh


def _conv_sgu(inp, rev):
    cw = np.asarray(inp["cm_conv_w"], np.float32)
    sw = np.asarray(inp["cm_sgu_w"], np.float32)
    sb_ = np.asarray(inp["cm_sgu_b"], np.float32)
    if rev:
        cw = cw[:, ::-1, :]
        sw = sw[:, :, ::-1, ::-1]
        sb_ = sb_[:, :, ::-1]
    convw = np.stack([np.ascontiguousarray(cw[j].T.reshape(4, 128, 31).transpose(1, 0, 2)).reshape(128, 124) for j in range(2)])
    sguw = np.stack([np.ascontiguousarray(sw[j].transpose(2, 0, 1)).reshape(128, 512) for j in range(2)])
    sgub = np.ascontiguousarray(np.tile(np.tile(sb_.reshape(2, 4, 1, 128), (1, 1, 4, 1)).reshape(2, 1, 2048), (1, 128, 1)))
    return np.ascontiguousarray(convw), np.ascontiguousarray(sguw), sgub


def _rope_tables(rev):
    t = np.arange(NTS)
    g = (4095 - t) if rev else t
    row = (g // 64).astype(np.float32)
    col = (g % 64).astype(np.float32)
    invf = np.power(np.float32(10000.0), -np.arange(0, 32, 2, dtype=np.float32) / np.float32(32)).astype(np.float32)
    cs = np.zeros((128, 2, NTS), np.float32)
    for p in range(128):
        d = p % 64
        if d < 32:
            ang = row * invf[d % 16]
            sign = -1.0 if d < 16 else 1.0
        else:
            ang = col * invf[(d - 32) % 16]
            sign = -1.0 if (d - 32) < 16 else 1.0
        ang = ang.astype(np.float32)
        cs[p, 0] = np.cos(ang)
        cs[p, 1] = sign * np.sin(ang)
    return np.ascontiguousarray(cs.reshape(128, 2 * NTS))


def _prep_core(inp, sh, i, cache):
    b, half = i // 2, i % 2
    m = dict(sh)
    xp = np.asarray(inp["x_prompt"], np.float32)[4 * i:4 * i + 4].reshape(NTP, D)
    m["xpT"] = np.ascontiguousarray(xp.T)
    xs = np.asarray(inp["x_sample"], np.float32)[b]
    xs = xs[::-1][:NTS] if half else xs[:NTS]
    m["xsT"] = np.ascontiguousarray(xs.T)
    cond = np.stack([_fm(inp["c_ctx"], 8), _fm(inp["c"][b], 8)], axis=2).reshape(128, 16)
    m["condT"] = np.ascontiguousarray(cond)
    for rev in (0, 1):
        if ("cs", rev) not in cache:
            cache[("cs", rev)] = _conv_sgu(inp, rev), _rope_tables(rev)
    (cw_p, sw_p, sb_p), _ = cache[("cs", 0)]
    (cw_s, sw_s, sb_s), cs = cache[("cs", half)]
    m.update(convw_p=cw_p, sguw_p=sw_p, sgub_p=sb_p, convw_s=cw_s, sguw_s=sw_s, sgub_s=sb_s, cs=cs)
    ck = np.asarray(inp["cache_k"], np.float32)[b]
    cvv = np.asarray(inp["cache_v"], np.float32)[b]
    kt = ck.transpose(0, 3, 2, 1)
    m["kcT"] = np.ascontiguousarray(np.concatenate([kt, kt], axis=1).reshape(2, 128, 1024))
    m["vc"] = np.ascontiguousarray(cvv.reshape(2, 2, 128, 256).transpose(0, 2, 1, 3).reshape(2, 128, 512))
    return m


_NC_CACHE = {}


def kernel(**inputs):
    key = "full"
    if key not in _NC_CACHE:
        _NC_CACHE[key] = build()
    nc = _NC_CACHE[key]
    sh = _prep_shared(inputs)
    cache = {}
    in_maps = [_prep_core(inputs, sh, i, cache) for i in range(8)]
    res = run_bass_kernel_spmd(nc, in_maps, core_ids=list(range(8)))
    yp = np.zeros((32, 256, D), np.float32)
    ys = np.zeros((4, 4096, D), np.float32)
    nk = np.zeros((32, 2, 256, 4, 64), np.float32)
    nv = np.zeros((32, 2, 256, 4, 64), np.float32)
    for i in range(8):
        r = res.results[i]
        b, half = i // 2, i % 2
        yp[4 * i:4 * i + 4] = np.asarray(r["ypT"]).T.reshape(4, 256, D)
        o = np.asarray(r["ysT"]).T
        if half:
            ys[b, 2048:] = o[::-1]
        else:
            ys[b, :2048] = o
        nk[4 * i:4 * i + 4] = np.asarray(r["nk"]).reshape(2, 4, 256, 4, 64).transpose(1, 0, 2, 3, 4)
        nv[4 * i:4 * i + 4] = np.asarray(r["nv"]).reshape(2, 4, 256, 4, 64).transpose(1, 0, 2, 3, 4)
    return yp, ys, nk, nv
```

```python
import os
import numpy as np
from contextlib import ExitStack
import concourse.bass as bass
import concourse.mybir as mybir
from concourse.bass_utils import run_bass_kernel_spmd

F32 = mybir.dt.float32
BF16 = mybir.dt.bfloat16
AF = mybir.ActivationFunctionType
ALU = mybir.AluOpType

D = 1024
DFF = 2816
NFC = 22
DEPTH = 4
NTP = 1024
NTS = 2560
NOWN = 2048
EPS = 1e-6
KQ = 8
NWA = 6
NWD = 2
SCRW = 19200
USE_GELU_TANH = False
SHRINK = True
PREPRE = True


class Buf:
    __slots__ = ("name", "w", "r")

    def __init__(self, name):
        self.name = name
        self.w = None
        self.r = []


class Op:
    __slots__ = ("eng", "fn", "deps", "dma", "sig", "sigval", "dn", "idx", "waits")


class Prog:
    def __init__(self):
        self.ops = []
        self.ndma = {"sp": 0, "pool": 0}
        self.dmaops = {"sp": [], "pool": []}
        self.last = {}
        self.pend = {}

    def add(self, eng, fn, reads=(), writes=(), dma=False, exempt=False):
        op = Op()
        op.eng = eng
        op.fn = fn
        op.dma = dma
        op.sig = False
        op.sigval = 0
        op.idx = len(self.ops)
        deps = set()
        for b in reads:
            if b.w is not None:
                deps.add(b.w)
        for b in writes:
            if b.w is not None:
                deps.add(b.w)
            for r in b.r:
                deps.add(r)
        for b in writes:
            b.w = op
            b.r = []
        for b in reads:
            if not dma:
                b.r = [o for o in b.r if o.dma or o.eng != eng]
            b.r.append(op)
        if dma:
            n = self.ndma[eng]
            self.ndma[eng] += 1
            op.dn = n
            if n >= KQ:
                deps.add(self.dmaops[eng][n - KQ])
            self.dmaops[eng].append(op)
        if eng in self.pend and not exempt:
            deps |= self.pend.pop(eng)
        deps.discard(op)
        op.deps = deps
        self.ops.append(op)
        if not dma:
            self.last[eng] = op
        return op

    def barrier(self):
        src = set(self.last.values())
        for q in ("sp", "pool"):
            src |= set(self.dmaops[q][-KQ:])
        for e in ("act", "dve", "pool", "sp"):
            self.pend[e] = set(src) | self.pend.get(e, set())

    def finalize(self):
        src = set()
        for q in ("sp", "pool"):
            src |= set(self.dmaops[q][-KQ:])
        op = self.add("sp", None)
        op.deps |= src
        seen = {}
        for op in self.ops:
            ws = []
            for d in sorted(op.deps, key=lambda o: o.idx):
                if d.dma:
                    key = ("dma", d.eng, d.dn % KQ)
                    val = d.dn // KQ + 1
                else:
                    if d.eng == op.eng and not op.dma and op.eng == "pe":
                        continue
                    key = ("eng", d.eng)
                    val = d.idx
                sk = (op.eng, key)
                if seen.get(sk, -1) >= val:
                    continue
                seen[sk] = val
                ws.append(d)
                if not d.dma:
                    d.sig = True
            op.waits = ws
        cnt = {}
        for op in self.ops:
            if op.sig:
                cnt[op.eng] = cnt.get(op.eng, 0) + 1
                op.sigval = cnt[op.eng]
        return cnt

    def emit(self, engname, eng, esem, dsem):
        for op in self.ops:
            if op.eng != engname:
                continue
            for d in op.waits:
                if d.dma:
                    eng.wait_ge(dsem[d.eng][d.dn % KQ], 16 * (d.dn // KQ + 1))
                else:
                    eng.wait_ge(esem[d.eng], d.sigval)
            if op.fn is None:
                continue
            ins = op.fn(eng)
            if op.dma:
                ins.then_inc(dsem[op.eng][op.dn % KQ], 16)
            elif op.sig:
                ins.then_inc(esem[op.eng], 1)


def build(depth=DEPTH, phases=("P", "S"), do_mixer=True, dbg=False, mix=("conv", "grp", "attn", "att", "op")):
    nc = bass.Bass("TRN2", target_bir_lowering=False)
    P = Prog()

    def din(name, shape):
        return nc.dram_tensor(name, list(shape), F32, kind="ExternalInput").ap()

    def dout(name, shape):
        return nc.dram_tensor(name, list(shape), F32, kind="ExternalOutput").ap()

    xT = {"P": din("xpT", (D, NTP)), "S": din("xsT", (D, NTS))}
    yT = {"P": dout("ypT", (D, NTP)), "S": dout("ysT", (D, NOWN))}
    nk_o = dout("nk", (2, 4, 256, 256))
    nv_o = dout("nv", (2, 4, 256, 256))
    w_mod = din("w_mod", (DEPTH, D, 9 * D))
    w_gate = din("ffn_w_gate", (DEPTH, 2, D, DFF))
    w_up = din("ffn_w_up", (DEPTH, 2, D, DFF))
    w_down = din("ffn_w_down", (DEPTH, 2, DFF, D))
    w_in = din("cm_w_in", (2, D, 2048))
    w_out = din("cm_w_out", (2, D, D))
    w_att = din("w_att", (2, D, 3584))
    w_o = din("attn_w_o", (2, D, D))
    bmodT_d = din("bmodT", (128, 288))
    normwT_d = din("normwT", (128, 192))
    cond_d = din("condT", (128, 16))
    convw_d = {"P": din("convw_p", (2, 128, 124)), "S": din("convw_s", (2, 128, 124))}
    convv_d = din("convv", (2, 128, 12))
    slnv_d = din("slnv", (2, 128, 1024))
    sguw_d = {"P": din("sguw_p", (2, 128, 512)), "S": din("sguw_s", (2, 128, 512))}
    sgub_d = {"P": din("sgub_p", (2, 128, 2048)), "S": din("sgub_s", (2, 128, 2048))}
    sink_d = din("sinkrow", (2, 128, 2048))
    kcT_d = din("kcT", (2, 128, 1024))
    vc_d = din("vc", (2, 128, 512))
    cs_d = din("cs", (128, 2 * NTS))
    mask_d = din("masks", (128, 1024))
    ident_d = din("ident", (128, 128))

    dbg_o = dout("dbg", (128, 2048)) if dbg else None
    es = ExitStack()
    with es:
        def sb(name, shape, dt):
            return es.enter_context(nc.sbuf_tensor(name, list(shape), dt))

        x = sb("x", (128, 8, NTS), F32)
        XB = [Buf(f"x{i}") for i in range(NTS // 128)]
        scr = sb("scr", (128, SCRW if not dbg else SCRW - 2200), F32)
        h = sb("h", (128, 8, 512), BF16)
        HB = Buf("h")
        wa = sb("wa", (128, NWA, 8, 256), BF16)
        WAB = [Buf(f"wa{i}") for i in range(NWA)]
        wd = sb("wd", (128, NWD, NFC, 128), BF16)
        WDB = [Buf(f"wd{i}") for i in range(NWD)]
        ones = sb("ones", (128, 128), BF16)
        ONESB = Buf("ones")
        e0 = sb("e0", (128, 128), BF16)
        ident = sb("ident_s", (128, 128), BF16)
        bmodT = sb("bmodT_s", (128, 288), F32)
        normwT = sb("normwT_s", (128, 192), F32)
        condr = sb("condr", (128, 16), F32)
        condb = sb("condb", (128, 8, 2), BF16)
        modsb = sb("modsb", (128, DEPTH, 2, 72), F32)
        coefA = sb("coefA", (128, DEPTH, 2, 3, 8), F32)
        coefG = sb("coefG", (128, DEPTH, 2, 3, 8), F32)
        CONSTB = Buf("const")
        masks = sb("masks_s", (128, 1024), BF16)
        ps = es.enter_context(nc.psum_tensor("ps", [128, 8, 512], F32))
        PSB = [Buf(f"ps{i}") for i in range(8)]
        esem = {e: es.enter_context(nc.semaphore(f"se_{e}")) for e in ("pe", "act", "dve", "pool")}
        dsem = {q: [es.enter_context(nc.semaphore(f"sd_{q}{i}")) for i in range(KQ)] for q in ("sp", "pool")}

        st = {"bank": 0, "wa": 0, "wd": 0}

        def bank():
            i = st["bank"]
            st["bank"] = (i + 1) % 6
            return ps[:, i, :], PSB[i]

        def sbank():
            i = 6 + st.get("sbank", 0)
            st["sbank"] = (i - 6 + 1) % 2
            return ps[:, i, :], PSB[i]

        class Carver:
            def __init__(self):
                self.off = 0

            def f32(self, n, *dims):
                ap = scr[:, self.off:self.off + n]
                self.off += n
                assert self.off <= SCRW, self.off
                return ap

            def bf(self, n):
                assert n % 2 == 0
                ap = scr[:, self.off:self.off + n // 2].bitcast(BF16)
                self.off += n // 2
                assert self.off <= SCRW, self.off
                return ap

        def load_wa(src2d, col0, ncols):
            i = st["wa"]
            st["wa"] = (i + 1) % NWA
            dst = wa[:, i, :, 0:ncols]
            src = src2d.rearrange("(k p) n -> p k n", p=128)[:, :, col0:col0 + ncols]
            P.add("pool", lambda e: e.dma_start(out=dst, in_=src), (), (WAB[i],), dma=True, exempt=True)
            return wa[:, i], WAB[i]

        def load_wd(src2d, col0):
            i = st["wd"]
            st["wd"] = (i + 1) % NWD
            dst = wd[:, i, :, :]
            src = src2d.rearrange("(f p) n -> p f n", p=128)[:, :, col0:col0 + 128]
            P.add("pool", lambda e: e.dma_start(out=dst, in_=src), (), (WDB[i],), dma=True, exempt=True)
            return wd[:, i], WDB[i]

        P.add("dve", lambda e: e.memset(ones[:], 1.0), (), (ONESB,))
        P.add("dve", lambda e: e.memset(e0[:], 0.0), (), (ONESB,))
        P.add("dve", lambda e: e.memset(e0[0:1, :], 1.0), (), (ONESB,))
        P.add("sp", lambda e: e.dma_start(out=bmodT[:], in_=bmodT_d), (), (CONSTB,), dma=True)
        P.add("sp", lambda e: e.dma_start(out=normwT[:], in_=normwT_d), (), (CONSTB,), dma=True)
        P.add("sp", lambda e: e.dma_start(out=condr[:], in_=cond_d), (), (CONSTB,), dma=True)
        P.add("pool", lambda e: e.dma_start(out=masks[:], in_=mask_d), (), (CONSTB,), dma=True)
        P.add("pool", lambda e: e.dma_start(out=ident[:], in_=ident_d), (), (CONSTB,), dma=True)
        P.add("act", lambda e: e.activation(out=condb[:].rearrange("p k r -> p (k r)"), in_=condr[:], func=AF.Silu),
              (CONSTB,), (CONSTB,))
        for l in range(depth):
            mp, mpb = bank()
            mpv = mp[:, 0:144].rearrange("p (f r) -> p f r", r=2)
            for pc in range(36):
                wt, wb = load_wa(w_mod[l], pc * 256, 256)
                for fi in range(2):
                    fc = pc * 2 + fi
                    for k in range(8):
                        P.add("pe", (lambda o, a, b, k=k: lambda e: e.matmul(o, lhsT=a, rhs=b, start=(k == 0), stop=(k == 7)))(
                            mpv[:, fc, :], wt[:, k, fi * 128:(fi + 1) * 128], condb[:, k, :]),
                            (wb, CONSTB), (mpb,))
            for r in range(2):
                P.add("dve", (lambda o, a, b: lambda e: e.tensor_tensor(out=o, in0=a, in1=b, op=ALU.add))(
                    modsb[:, l, r, :], mpv[:, :, r], bmodT[:, l * 72:(l + 1) * 72]), (mpb, CONSTB), (CONSTB,))
                for s in range(3):
                    wgt = 1.0 if s == 1 else 0.5
                    P.add("dve", (lambda o, a, b: lambda e: e.scalar_tensor_tensor(out=o, in0=a, scalar=1.0, in1=b, op0=ALU.add, op1=ALU.mult))(
                        coefA[:, l, r, s, :], modsb[:, l, r, (3 * s + 1) * 8:(3 * s + 2) * 8],
                        normwT[:, l * 48 + 2 * s * 8: l * 48 + 2 * s * 8 + 8]), (CONSTB,), (CONSTB,))
                    P.add("dve", (lambda o, a, b, w: lambda e: e.scalar_tensor_tensor(out=o, in0=a, scalar=w, in1=b, op0=ALU.mult, op1=ALU.mult))(
                        coefG[:, l, r, s, :], modsb[:, l, r, (3 * s + 2) * 8:(3 * s + 3) * 8],
                        normwT[:, l * 48 + (2 * s + 1) * 8: l * 48 + (2 * s + 1) * 8 + 8], wgt), (CONSTB,), (CONSTB,))
        P.barrier()

        def xbufs(tok0, n):
            return [XB[b] for b in range(tok0 // 128, (tok0 + n + 127) // 128)]

        def rstd_from(ssb, ssp, n, rs_ap, rsB, ndim):
            P.add("act", lambda e: e.activation(out=rs_ap[:, 0:n], in_=ssp[:, 0:n], func=AF.Sqrt, bias=epsc[:, 0:1], scale=1.0 / ndim),
                  (ssb, CONSTB), (rsB,))
            P.add("dve", lambda e: e.reciprocal(out=rs_ap[:, 0:n], in_=rs_ap[:, 0:n]), (rsB,), (rsB,))

        prepre = {"k": None, "next": None}

        def pre(l, r, s, tok0, n, K, stage="ab"):
            if stage == "ab" and prepre.get("k") == (l, s, tok0, n):
                prepre["k"] = None
                return
            xb = xbufs(tok0, n)
            rs_, rsB_ = K.get("rs2", K["rs"]), K.get("rs2B", K["rsB"])
            sq_, sqB_ = K.get("sqp", K["sq"]), K.get("sqpB", K["sqB"])

            def square(c):
                q = c % 4
                P.add("act", (lambda o, a: lambda e: e.activation(out=o, in_=a, func=AF.Square))(
                    sq_[:, q * 512:q * 512 + n], x[:, c, tok0:tok0 + n]), xb, (sqB_[q],))

            if "a" in stage:
                for c in range(4):
                    square(c)
                if stage == "a":
                    return
            ssp, ssb = sbank()
            for c in range(8):
                q = c % 4
                if c >= 4:
                    square(c)
                P.add("pe", (lambda o, b, c=c: lambda e: e.matmul(o, lhsT=ones[:], rhs=b, start=(c == 0), stop=(c == 7)))(
                    ssp[:, 0:n], sq_[:, q * 512:q * 512 + n]), (sqB_[q], ONESB), (ssb,))
            rstd_from(ssb, ssp, n, rs_, rsB_, D)
            for c in range(8):
                q = c % 2
                P.add("dve", (lambda o, a, sc, b: lambda e: e.scalar_tensor_tensor(out=o, in0=a, scalar=sc, in1=b, op0=ALU.mult, op1=ALU.mult))(
                    K["tmp"][:, q * 512:q * 512 + n], x[:, c, tok0:tok0 + n], coefA[:, l, r, s, c:c + 1], rs_[:, 0:n]),
                    xb + [rsB_, CONSTB], (K["tmpB"][q],))
                P.add("act", (lambda o, a, b: lambda e: e.activation(out=o, in_=a, func=AF.Identity, bias=b, scale=1.0))(
                    h[:, c, 0:n], K["tmp"][:, q * 512:q * 512 + n], modsb[:, l, r, 3 * s * 8 + c: 3 * s * 8 + c + 1]),
                    (K["tmpB"][q], CONSTB), (HB,))

        def post(l, r, s, tok0, n, K):
            xb = xbufs(tok0, n)
            rstd_from(K["ssb"], K["ssp"], n, K["rs"], K["rsB"], D)
            for c in range(8):
                q = c % 2
                P.add("dve", (lambda o, a, sc, b: lambda e: e.scalar_tensor_tensor(out=o, in0=a, scalar=sc, in1=b, op0=ALU.mult, op1=ALU.mult))(
                    K["tmp"][:, q * 512:q * 512 + n], K["ys"][:, c * 512:c * 512 + n], coefG[:, l, r, s, c:c + 1], K["rs"][:, 0:n]),
                    (K["ysB"], K["rsB"], CONSTB), (K["tmpB"][q],))
                P.add("dve", (lambda o, a, b: lambda e: e.tensor_tensor(out=o, in0=a, in1=b, op=ALU.add))(
                    x[:, c, tok0:tok0 + n], x[:, c, tok0:tok0 + n], K["tmp"][:, q * 512:q * 512 + n]),
                    xb + [K["tmpB"][q]], xb)

        def evac_y(K, yp, ypb, d, n, pend):
            q = d % 4
            P.add("act", (lambda o, a: lambda e: e.activation(out=o, in_=a, func=AF.Copy))(
                K["ys"][:, d * 512:d * 512 + n], yp[:, 0:n]), (ypb,), (K["ysB"],))
            P.add("act", (lambda o, a: lambda e: e.activation(out=o, in_=a, func=AF.Square))(
                K["sq"][:, q * 512:q * 512 + n], yp[:, 0:n]), (ypb,), (K["sqB"][q],))
            pend.append((d, q))

        def flush_ss(K, n, pend):
            while pend:
                d, q = pend.pop(0)
                P.add("pe", (lambda o, b, d=d: lambda e: e.matmul(o, lhsT=ones[:], rhs=b, start=(d == 0), stop=(d == 7)))(
                    K["ssp"][:, 0:n], K["sq"][:, q * 512:q * 512 + n]), (K["sqB"][q], ONESB), (K["ssb"],))

        def outproj_post(l, r, s, wsrc, tok0, n, K, last=False):
            K["ssp"], K["ssb"] = sbank()
            pend = []
            for pc in range(4):
                wt, wb = load_wa(wsrc, pc * 256, 256)
                for di in range(2):
                    d = pc * 2 + di
                    yp, ypb = bank()
                    for k in range(8):
                        P.add("pe", (lambda o, a, b, k=k: lambda e: e.matmul(o, lhsT=a, rhs=b, start=(k == 0), stop=(k == 7)))(
                            yp[:, 0:n], wt[:, k, di * 128:(di + 1) * 128], h[:, k, 0:n]), (wb, HB), (ypb,))
                    flush_ss(K, n, pend)
                    evac_y(K, yp, ypb, d, n, pend)
            flush_ss(K, n, pend)
            if last and prepre.get("mixnext") is not None:
                (l2, s2, n2) = prepre["mixnext"]
                prepre["mixnext"] = None
                pre(l2, r, s2, 0, n2, K)
                prepre["k"] = (l2, s2, 0, n2)
            post(l, r, s, tok0, n, K)

        epsc = sb("epsc", (128, 1), F32)
        P.add("dve", lambda e: e.memset(epsc[:], EPS), (), (CONSTB,))

        def ffn_carve():
            C = Carver()
            K = {}
            K["a"] = C.bf(NFC * 512)
            K["aB"] = [Buf(f"a{f}") for f in range(NFC)]
            K["ys"] = C.f32(8 * 512)
            K["ysB"] = Buf("ys")
            K["tmp"] = C.f32(2 * 512)
            K["tmpB"] = [Buf("tmp0"), Buf("tmp1")]
            K["sq"] = C.bf(4 * 512)
            K["sqB"] = [Buf(f"sq{i}") for i in range(4)]
            K["sg"] = C.f32(2 * 512)
            K["sgB"] = [Buf("sg0"), Buf("sg1")]
            K["rs"] = C.f32(512)
            K["rsB"] = Buf("rs")
            K["rs2"] = C.f32(512)
            K["rs2B"] = Buf("rs2")
            K["sqp"] = C.bf(4 * 512)
            K["sqpB"] = [Buf(f"sqp{i}") for i in range(4)]
            return K

        dbgt = sb("dbgt", (128, 2048), F32) if dbg else None
        DBGB = Buf("dbg")
        dstate = {"done": False}

        def ffn_gateup(l, j, r, tok0, n, K):
            for pc in range(11):
                wg, wgb = load_wa(w_gate[l, j], pc * 256, 256)
                wu, wub = load_wa(w_up[l, j], pc * 256, 256)
                for fi in range(2):
                    f = pc * 2 + fi
                    gp, gpb = bank()
                    up, upb = bank()
                    for k in range(8):
                        P.add("pe", (lambda o, a, b, k=k: lambda e: e.matmul(o, lhsT=a, rhs=b, start=(k == 0), stop=(k == 7)))(
                            gp[:, 0:n], wg[:, k, fi * 128:(fi + 1) * 128], h[:, k, 0:n]), (wgb, HB), (gpb,))
                    for k in range(8):
                        P.add("pe", (lambda o, a, b, k=k: lambda e: e.matmul(o, lhsT=a, rhs=b, start=(k == 0), stop=(k == 7)))(
                            up[:, 0:n], wu[:, k, fi * 128:(fi + 1) * 128], h[:, k, 0:n]), (wub, HB), (upb,))
                    q = f % 2
                    P.add("act", (lambda o, a: lambda e: e.activation(out=o, in_=a, func=AF.Silu))(
                        K["sg"][:, q * 512:q * 512 + n], gp[:, 0:n]), (gpb,), (K["sgB"][q],))
                    P.add("dve", (lambda o, a, b: lambda e: e.tensor_tensor(out=o, in0=a, in1=b, op=ALU.mult))(
                        K["a"][:, f * 512:f * 512 + n], up[:, 0:n], K["sg"][:, q * 512:q * 512 + n]),
                        (upb, K["sgB"][q]), (K["aB"][f],))

        def ffn_down(l, j, r, tok0, n, K, mid_hook=None):
            s = 0 if j == 0 else 2
            K["ssp"], K["ssb"] = sbank()
            pend = []
            for d in range(8):
                wt, wb = load_wd(w_down[l, j], d * 128)
                yp, ypb = bank()
                for f in range(NFC):
                    P.add("pe", (lambda o, a, b, f=f: lambda e: e.matmul(o, lhsT=a, rhs=b, start=(f == 0), stop=(f == NFC - 1)))(
                        yp[:, 0:n], wt[:, f, :], K["a"][:, f * 512:f * 512 + n]), (wb, K["aB"][f]), (ypb,))
                flush_ss(K, n, pend)
                evac_y(K, yp, ypb, d, n, pend)
                if d == 1 and mid_hook is not None:
                    mid_hook()
            flush_ss(K, n, pend)
            post(l, r, s, tok0, n, K)

        def ffn_sublayer(l, j, r, tiles, K):
            s = 0 if j == 0 else 2
            pre(l, r, s, tiles[0][0], tiles[0][1], K)
            for ti, (t0, n) in enumerate(tiles):
                ffn_gateup(l, j, r, t0, n, K)
                hook = None
                if ti + 1 < len(tiles):
                    nt0, nn = tiles[ti + 1]
                    pre(l, r, s, nt0, nn, K, stage="a")
                    hook = (lambda nt0=nt0, nn=nn: pre(l, r, s, nt0, nn, K, stage="b"))
                elif prepre["next"] is not None and len(tiles) > 1:
                    (l2, s2, n2) = prepre["next"]
                    pre(l2, r, s2, 0, n2, K, stage="a")

                    def hook(l2=l2, s2=s2, n2=n2):
                        pre(l2, r, s2, 0, n2, K, stage="b")
                        prepre["k"] = (l2, s2, 0, n2)
                ffn_down(l, j, r, t0, n, K, mid_hook=hook)

        def A_(eng, fn, R=(), W=(), dma=False):
            return P.add(eng, fn, tuple(R), tuple(W), dma=dma)

        def mmk(o, a, b, first, last, R, W):
            A_("pe", lambda e: e.matmul(o, lhsT=a, rhs=b, start=first, stop=last), R, W)

        def proj_fm(wt, wb, c0, n, out_ps, out_b):
            for k in range(8):
                mmk(out_ps[:, 0:n], wt[:, k, c0:c0 + 128], h[:, k, 0:n], k == 0, k == 7, (wb, HB), (out_b,))

        def attn_layer(l, ph, r, NT, flush=True):
            j = l // 2
            NB = NT // 128
            C = Carver()
            K = {}
            kT = C.bf(4 * 1024); KB = [Buf(f"k{i}") for i in range(8)]
            vt = C.bf(8 * 256); VB = [Buf(f"v{i}") for i in range(8)]
            qT = C.bf(8 * 768); QB = [Buf(f"q{i}") for i in range(6)]
            cs = C.bf(2 * NTS); CSB = Buf("cs")
            NPT = 5
            pT = C.bf(NPT * 512); PTB = [Buf(f"pt{i}") for i in range(NPT)]
            K["sq"] = C.bf(4 * 512); K["sqB"] = [Buf(f"sq{i}") for i in range(4)]
            K["tmp"] = C.f32(2 * 512); K["tmpB"] = [Buf("tmp0"), Buf("tmp1")]
            K["rs"] = C.f32(512); K["rsB"] = Buf("rs")
            K["ys"] = C.f32(8 * 512); K["ysB"] = Buf("ys")
            kc = C.bf(4 * 256); vcx = C.bf(2 * 256); esk = C.bf(2048); LB = Buf("lconst")
            rden = C.f32(512); RDB = Buf("rden")
            ptc = {"i": 0}
            A_("pool", lambda e: e.dma_start(out=kc, in_=kcT_d[j]), (), (LB,), dma=True)
            A_("pool", lambda e: e.dma_start(out=vcx, in_=vc_d[j]), (), (LB,), dma=True)
            A_("pool", lambda e: e.dma_start(out=esk, in_=sink_d[j]), (), (LB,), dma=True)
            A_("act", lambda e: e.activation(out=esk, in_=esk, func=AF.Exp), (LB,), (LB,))
            if ph == "S":
                A_("pool", lambda e: e.dma_start(out=cs, in_=cs_d), (), (CSB,), dma=True)

            def rope_or_copy(src_ps, srcb, perm_ps, permb, tok0, n, dsts):
                if ph == "P":
                    for (dst, db, c0, ncol) in dsts:
                        A_("act", (lambda o, a: lambda e: e.activation(out=o, in_=a, func=AF.Copy))(dst, src_ps[:, c0:c0 + ncol]), (srcb,), (db,))
                    return
                t1 = K["tmp"][:, 0:n]
                t2 = K["tmp"][:, 512:512 + n]
                A_("dve", lambda e: e.tensor_tensor(out=t1, in0=src_ps[:, 0:n], in1=cs[:, tok0:tok0 + n], op=ALU.mult), (srcb, CSB), (K["tmpB"][0],))
                A_("dve", lambda e: e.tensor_tensor(out=t2, in0=perm_ps[:, 0:n], in1=cs[:, NTS + tok0:NTS + tok0 + n], op=ALU.mult), (permb, CSB), (K["tmpB"][1],))
                for (dst, db, c0, ncol) in dsts:
                    A_("dve", (lambda o, a, b: lambda e: e.tensor_tensor(out=o, in0=a, in1=b, op=ALU.add))(
                        dst, K["tmp"][:, c0:c0 + ncol], K["tmp"][:, 512 + c0:512 + c0 + ncol]), K["tmpB"], (db,))

            STAGE = int(os.environ.get("ATT_STAGE", "99"))

            def project(tok0, n):
                b0 = tok0 // 128
                nb = n // 128
                if STAGE < 1:
                    return
                pre(l, r, 1, tok0, n, K)
                for pc in range(4 if STAGE >= 2 else 0):
                    wq, wqb = load_wa(w_att[j], pc * 256, 256)
                    if ph == "S":
                        wp, wpb = load_wa(w_att[j], 1024 + pc * 256, 256)
                    for ci in range(2):
                        c = pc * 2 + ci
                        qp, qpb = bank()
                        proj_fm(wq, wqb, ci * 128, n, qp, qpb)
                        pp, ppb = (None, None)
                        if ph == "S":
                            pp, ppb = bank()
                            proj_fm(wp, wpb, ci * 128, n, pp, ppb)
                        dsts = [(qT[:, c * 768 + ((b0 + bi) % 6) * 128: c * 768 + ((b0 + bi) % 6) * 128 + 128], QB[(b0 + bi) % 6], bi * 128, 128)
                                for bi in range(nb)]
                        rope_or_copy(qp, qpb, pp, ppb, tok0, n, dsts)
                ks0 = (b0 % 8)
                for pc in range(2 if STAGE >= 3 else 0):
                    wk, wkb = load_wa(w_att[j], 2048 + pc * 256, 256)
                    if ph == "S":
                        wp, wpb = load_wa(w_att[j], 2560 + pc * 256, 256)
                    for ci in range(2):
                        kv = pc * 2 + ci
                        kp, kpb = bank()
                        proj_fm(wk, wkb, ci * 128, n, kp, kpb)
                        pp, ppb = (None, None)
                        if ph == "S":
                            pp, ppb = bank()
                            proj_fm(wp, wpb, ci * 128, n, pp, ppb)
                        dsts = [(kT[:, kv * 1024 + ks0 * 128: kv * 1024 + ks0 * 128 + n], None, 0, n)]
                        if ph == "P":
                            A_("act", (lambda o, a: lambda e: e.activation(out=o, in_=a, func=AF.Copy))(dsts[0][0], kp[:, 0:n]), (kpb,), KB[ks0:ks0 + nb])
                        else:
                            t1 = K["tmp"][:, 0:n]
                            t2 = K["tmp"][:, 512:512 + n]
                            A_("dve", (lambda a: lambda e: e.tensor_tensor(out=t1, in0=a, in1=cs[:, tok0:tok0 + n], op=ALU.mult))(kp[:, 0:n]), (kpb, CSB), (K["tmpB"][0],))
                            A_("dve", (lambda a: lambda e: e.tensor_tensor(out=t2, in0=a, in1=cs[:, NTS + tok0:NTS + tok0 + n], op=ALU.mult))(pp[:, 0:n]), (ppb, CSB), (K["tmpB"][1],))
                            A_("dve", (lambda o: lambda e: e.tensor_tensor(out=o, in0=t1, in1=t2, op=ALU.add))(dsts[0][0]), K["tmpB"], KB[ks0:ks0 + nb])
                if STAGE < 4:
                    return
                if ph == "P":
                    wkk, wkkb = load_wa(w_att[j], 3072, 256)
                wvv, wvvb = load_wa(w_att[j], 3328, 256)
                for bi in range(nb):
                    blk = b0 + bi
                    vp, vpb = bank()
                    if ph == "P":
                        vp2, vpb2 = bank()
                        for k in range(8):
                            mmk(vp[:, 0:256], h[:, k, bi * 128:(bi + 1) * 128], wkk[:, k, :], k == 0, k == 7, (HB, wkkb), (vpb,))
                        for k in range(8):
                            mmk(vp2[:, 0:256], h[:, k, bi * 128:(bi + 1) * 128], wvv[:, k, :], k == 0, k == 7, (HB, wvvb), (vpb2,))
                        kvo = K["ys"][:, (bi % 2) * 512:(bi % 2) * 512 + 512]
                        A_("act", (lambda o, a: lambda e: e.activation(out=o, in_=a, func=AF.Copy))(kvo[:, 0:256], vp[:, 0:256]), (vpb,), (K["ysB"],))
                        A_("act", (lambda o, a: lambda e: e.activation(out=o, in_=a, func=AF.Copy))(kvo[:, 256:512], vp2[:, 0:256]), (vpb2,), (K["ysB"],))
                        if STAGE >= 5:
                            A_("sp", (lambda o, a: lambda e: e.dma_start(out=o, in_=a))(nk_o[j, blk // 2, (blk % 2) * 128:(blk % 2) * 128 + 128, :], kvo[:, 0:256]), (K["ysB"],), (), dma=True)
                            A_("sp", (lambda o, a: lambda e: e.dma_start(out=o, in_=a))(nv_o[j, blk // 2, (blk % 2) * 128:(blk % 2) * 128 + 128, :], kvo[:, 256:512]), (K["ysB"],), (), dma=True)
                        A_("dve", (lambda o, a: lambda e: e.tensor_copy(out=o, in_=a))(vt[:, (blk % 8) * 256:(blk % 8) * 256 + 256], kvo[:, 256:512]), (K["ysB"],), (VB[blk % 8],))
                    else:
                        for k in range(8):
                            mmk(vp[:, 0:256], h[:, k, bi * 128:(bi + 1) * 128], wvv[:, k, :], k == 0, k == 7, (HB, wvvb), (vpb,))
                        A_("act", (lambda o, a: lambda e: e.activation(out=o, in_=a, func=AF.Copy))(vt[:, (blk % 8) * 256:(blk % 8) * 256 + 256], vp[:, 0:256]), (vpb,), (VB[blk % 8],))

            AST = int(os.environ.get("ATT_A", "99"))

            def attend(i, keys, gi):
                qs = i % 6
                for kv in range(4):
                    pts = []
                    for (kind, kb, msk) in keys:
                        spA, spAb = bank()
                        spB, spBb = bank()
                        for hh in range(4):
                            c = 2 * kv + hh // 2
                            lo = (hh % 2) * 64
                            sp_, spb = (spA, spAb) if hh % 2 == 0 else (spB, spBb)
                            if kind == "l":
                                ksl = kT[lo:lo + 64, kv * 1024 + (kb % 8) * 128: kv * 1024 + (kb % 8) * 128 + 128]
                                kbuf = KB[kb % 8]
                            else:
                                ksl = kc[lo:lo + 64, kv * 256 + kb * 128: kv * 256 + kb * 128 + 128]
                                kbuf = LB
                            mmk(sp_[:, (hh // 2) * 128:(hh // 2 + 1) * 128], ksl, qT[lo:lo + 64, c * 768 + qs * 128: c * 768 + qs * 128 + 128], True, True,
                                (kbuf, QB[qs]), (spb,))
                        if AST < 2:
                            continue
                        pi = ptc["i"]
                        ptc["i"] = (pi + 1) % NPT
                        pt = pT[:, pi * 512:(pi + 1) * 512]
                        A_("act", (lambda o, a: lambda e: e.activation(out=o, in_=a, func=AF.Exp, scale=0.125))(pt[:, 0:256], spA[:, 0:256]), (spAb,), (PTB[pi],))
                        A_("act", (lambda o, a: lambda e: e.activation(out=o, in_=a, func=AF.Exp, scale=0.125))(pt[:, 256:512], spB[:, 0:256]), (spBb,), (PTB[pi],))
                        if msk:
                            mk = masks[:, (msk - 1) * 512: msk * 512]
                            A_("dve", (lambda o, m: lambda e: e.tensor_tensor(out=o, in0=o, in1=m, op=ALU.mult))(pt, mk), (PTB[pi], CONSTB), (PTB[pi],))
                        pts.append((pt, PTB[pi], kind, kb))
                    if AST < 3:
                        continue
                    rsp, rsb = sbank()
                    for n_, (pt, pb, kind, kb) in enumerate(pts):
                        mmk(rsp[:, :], ones[:, :], pt, n_ == 0, False, (pb, ONESB), (rsb,))
                    mmk(rsp[:, :], e0[:, :], esk[:, kv * 512:(kv + 1) * 512], False, True, (LB, ONESB), (rsb,))
                    if AST < 4:
                        continue
                    obanks = [bank(), bank()]
                    for hh in range(4):
                        c2 = hh // 2
                        op_, opb = obanks[c2]
                        lo = (hh % 2) * 64
                        for n_, (pt, pb, kind, kb) in enumerate(pts):
                            if kind == "l":
                                vsl = vt[:, (kb % 8) * 256 + kv * 64:(kb % 8) * 256 + kv * 64 + 64]
                                vbuf = VB[kb % 8]
                            else:
                                vsl = vcx[:, kb * 256 + kv * 64: kb * 256 + kv * 64 + 64]
                                vbuf = LB
                            mmk(op_[lo:lo + 64, 0:128], vsl, pt[:, (hh % 2) * 256 + (hh // 2) * 128:(hh % 2) * 256 + (hh // 2) * 128 + 128], n_ == 0, n_ == len(pts) - 1,
                                (pb, vbuf), (opb,))
                    if AST < 5:
                        continue
                    A_("dve", (lambda a: lambda e: e.reciprocal(out=rden[:, :], in_=a))(rsp[:, :]), (rsb,), (RDB,))
                    for hh in range(4):
                        c2 = hh // 2
                        op_, opb = obanks[c2]
                        lo = (hh % 2) * 64
                        A_("dve", (lambda o, a, b: lambda e: e.tensor_tensor(out=o, in0=a, in1=b, op=ALU.mult))(
                            h[lo:lo + 64, 2 * kv + c2, gi * 128:(gi + 1) * 128], op_[lo:lo + 64, 0:128], rden[lo:lo + 64, (hh % 2) * 256 + (hh // 2) * 128:(hh % 2) * 256 + (hh // 2) * 128 + 128]),
                            (opb, RDB), (HB,))

            def group(blocks, last=False):
                for gi, i in enumerate(blocks):
                    if ph == "P":
                        s0 = (i // 2) * 2
                        keys = [("l", s0, 0), ("l", s0 + 1, 0)]
                    else:
                        keys = []
                        if i - 1 >= 0:
                            keys.append(("l", i - 1, 1))
                        keys.append(("l", i, 0))
                        if i + 1 < NB:
                            keys.append(("l", i + 1, 2))
                        keys += [("c", 0, 0), ("c", 1, 0)]
                    if "att" in mix:
                        attend(i, keys, gi)
                if "op" in mix:
                    outproj_post(l, r, 1, w_o[j], blocks[0] * 128, len(blocks) * 128, K, last=last)

            for t0 in range(0, NT, 512):
                n = min(512, NT - t0)
                project(t0, n)
                b0 = t0 // 128
                lastg = (t0 + 512 >= NT) and not (ph == "S" and flush)
                if ph == "P":
                    group([b0, b0 + 1, b0 + 2, b0 + 3], last=lastg)
                else:
                    group([b for b in range(b0 - 1, b0 + n // 128 - 1) if b >= 0], last=lastg)
            if ph == "S" and flush:
                group([NB - 1])


        def gelu_tanh(srcs, dst, dstb, n, K, accum=None):
            xs = K["gl"][:, 0:n]
            tt = K["gl"][:, 512:512 + n]
            for (sp_ap, sbuf_, c0, ncol) in srcs:
                A_("act", (lambda o, a: lambda e: e.activation(out=o, in_=a, func=AF.Copy))(K["gl"][:, c0:c0 + ncol], sp_ap), (sbuf_,), (K["glB"][0],))
                A_("act", (lambda o, a: lambda e: e.activation(out=o, in_=a, func=AF.Square))(K["gl"][:, 512 + c0:512 + c0 + ncol], sp_ap), (sbuf_,), (K["glB"][1],))
            A_("dve", lambda e: e.tensor_scalar(out=tt, in0=tt, scalar1=0.044715, scalar2=1.0, op0=ALU.mult, op1=ALU.add), (K["glB"][1],), (K["glB"][1],))
            A_("dve", lambda e: e.tensor_tensor(out=tt, in0=tt, in1=xs, op=ALU.mult), K["glB"], (K["glB"][1],))
            A_("act", lambda e: e.activation(out=tt, in_=tt, func=AF.Sigmoid, scale=1.5957691216057308), (K["glB"][1],), (K["glB"][1],))
            if accum is None:
                A_("dve", lambda e: e.tensor_tensor(out=dst, in0=tt, in1=xs, op=ALU.mult), K["glB"], (dstb,))
            else:
                A_("dve", lambda e: e.scalar_tensor_tensor(out=dst, in0=tt, scalar=1.0, in1=xs, op0=ALU.mult, op1=ALU.mult, accum_out=accum), K["glB"], (dstb,))

        def conv_layer(l, ph, r, NT, flush=True):
            j = l // 2
            NB = NT // 128
            C = Carver()
            K = {}
            GW = 672
            G = C.bf(4 * GW); GB = Buf("G")
            NDG = 16
            dg = C.bf(NDG * 128); DGB = [Buf(f"dg{i}") for i in range(NDG)]
            dgc = {"i": 0}
            BO = C.bf(4 * 640); BOB = Buf("BO")
            vtg = C.bf(4 * 512); VGB = [Buf(f"vg{i}") for i in range(4)]
            sgm = C.f32(2 * 512); SGB = [Buf("sgm0"), Buf("sgm1")]
            K["gl"] = C.f32(2 * 512); K["glB"] = [Buf("gl0"), Buf("gl1")]
            cgb = C.bf(4 * 512); CGB = Buf("cg")
            mu = C.f32(512); var = C.f32(512); rstd = C.f32(512); STB = Buf("stat")
            K["ys"] = C.f32(8 * 512); K["ysB"] = Buf("ys")
            K["sq"] = C.bf(4 * 512); K["sqB"] = [Buf(f"sq{i}") for i in range(4)]
            cgq = K["sq"]
            K["tmp"] = C.f32(2 * 512); K["tmpB"] = [Buf("tmp0"), Buf("tmp1")]
            K["rs"] = C.f32(512); K["rsB"] = Buf("rs")
            slnv = C.f32(1024); convw = C.f32(124); convv = C.f32(12); sguw = C.bf(512); sgub = C.bf(2048); LB = Buf("lconst")
            sts = C.f32(8); SSB = Buf("sts")
            A_("sp", lambda e: e.dma_start(out=slnv, in_=slnv_d[j]), (), (LB,), dma=True)
            A_("sp", lambda e: e.dma_start(out=convw, in_=convw_d[ph][j]), (), (LB,), dma=True)
            A_("sp", lambda e: e.dma_start(out=convv, in_=convv_d[j]), (), (LB,), dma=True)
            A_("pool", lambda e: e.dma_start(out=sguw, in_=sguw_d[ph][j]), (), (LB,), dma=True)
            A_("pool", lambda e: e.dma_start(out=sgub, in_=sgub_d[ph][j]), (), (LB,), dma=True)
            A_("dve", lambda e: e.memset(G, 0.0), (), (GB,))

            def inproj(tok0, n, gsegs, bo0):
                nb = n // 128
                pre(l, r, 1, tok0, n, K)
                for pc in range(2):
                    wv_, wvb = load_wa(w_in[j], pc * 256, 256)
                    wg_, wgb = load_wa(w_in[j], 512 + pc * 256, 256)
                    for ci in range(2):
                        cc = pc * 2 + ci
                        avp, avb = bank()
                        agp, agb = bank()
                        proj_fm(wv_, wvb, ci * 128, n, avp, avb)
                        proj_fm(wg_, wgb, ci * 128, n, agp, agb)
                        q = cc % 2
                        A_("act", (lambda o, a: lambda e: e.activation(out=o, in_=a, func=AF.Sigmoid))(sgm[:, q * 512:q * 512 + n], agp[:, 0:n]), (agb,), (SGB[q],))
                        for (c0, ncol, gu0) in gsegs:
                            A_("dve", (lambda o, a, b: lambda e: e.tensor_tensor(out=o, in0=a, in1=b, op=ALU.mult))(
                                G[:, cc * GW + gu0: cc * GW + gu0 + ncol], avp[:, c0:c0 + ncol], sgm[:, q * 512 + c0:q * 512 + c0 + ncol]), (avb, SGB[q]), (GB,))
                wv0, wv0b = load_wa(w_in[j], 1536, 256)
                wv1, wv1b = load_wa(w_in[j], 1792, 256)
                for bi in range(nb):
                    vp, vpb = bank()
                    vp2, vpb2 = bank()
                    for k in range(8):
                        mmk(vp[:, 0:256], h[:, k, bi * 128:(bi + 1) * 128], wv0[:, k, :], k == 0, k == 7, (HB, wv0b), (vpb,))
                    for k in range(8):
                        mmk(vp2[:, 0:256], h[:, k, bi * 128:(bi + 1) * 128], wv1[:, k, :], k == 0, k == 7, (HB, wv1b), (vpb2,))
                    gq = sgm[:, (bi % 2) * 512:(bi % 2) * 512 + 512]
                    gqb = SGB[bi % 2]
                    gelu_tanh([(vp[:, 0:256], vpb, 0, 256), (vp2[:, 0:256], vpb2, 256, 256)], gq, gqb, 512, K, accum=sts[:, 0:1])
                    A_("dve", lambda e: e.tensor_scalar(out=sts[:, 1:2], in0=sts[:, 0:1], scalar1=-1.0 / 512, scalar2=None, op0=ALU.mult), (gqb,), (SSB,))
                    A_("act", (lambda a: lambda e: e.activation(out=K["gl"][:, 0:512], in_=a, func=AF.Square, bias=sts[:, 1:2], scale=1.0, accum_out=sts[:, 2:3]))(gq), (gqb, SSB), (K["glB"][0], SSB))
                    A_("act", lambda e: e.activation(out=sts[:, 3:4], in_=sts[:, 2:3], func=AF.Sqrt, bias=epsc[:, 0:1], scale=1.0 / 512), (SSB, CONSTB), (SSB,))
                    A_("dve", lambda e: e.reciprocal(out=sts[:, 3:4], in_=sts[:, 3:4]), (SSB,), (SSB,))
                    A_("dve", (lambda a: lambda e: e.tensor_scalar(out=a, in0=a, scalar1=sts[:, 1:2], scalar2=sts[:, 3:4], op0=ALU.add, op1=ALU.mult))(gq), (gqb, SSB), (gqb,))
                    A_("dve", (lambda a: lambda e: e.tensor_tensor(out=a, in0=a, in1=slnv[:, 0:512], op=ALU.mult))(gq), (gqb, LB), (gqb,))
                    A_("dve", (lambda o, a: lambda e: e.tensor_tensor(out=o, in0=a, in1=slnv[:, 512:1024], op=ALU.add))(vtg[:, bi * 512:(bi + 1) * 512], gq), (gqb, LB), (VGB[bi],))
                for pc in range(2):
                    wu_, wub = load_wa(w_in[j], 1024 + pc * 256, 256)
                    for ci in range(2):
                        g = pc * 2 + ci
                        up, upb = bank()
                        proj_fm(wu_, wub, ci * 128, n, up, upb)
                        ug = sgm[:, (g % 2) * 512:(g % 2) * 512 + n]
                        ugb = SGB[g % 2]
                        gelu_tanh([(up[:, 0:n], upb, 0, n)], ug, ugb, n, K)
                        spp, sppb = bank()
                        mmk(spp[:, 0:n], e0[:, :], sgub[:, g * 512: g * 512 + n], True, False, (ONESB, LB), (sppb,))
                        for bi in range(nb):
                            mmk(spp[:, bi * 128:(bi + 1) * 128], vtg[:, bi * 512 + g * 128: bi * 512 + g * 128 + 128], sguw[:, g * 128:(g + 1) * 128], False, bi == nb - 1, (VGB[bi], LB), (sppb,))
                        A_("dve", (lambda o, a, b: lambda e: e.tensor_tensor(out=o, in0=a, in1=b, op=ALU.mult))(
                            BO[:, g * 640 + bo0: g * 640 + bo0 + n], spp[:, 0:n], ug), (sppb, ugb), (BOB,))

            def group(tokg0, n, csegs, bo0, last=False):
                cps = [bank() for cc in range(4)]
                assert len(csegs) == 1
                (c0, ncol, gu0) = csegs[0]
                for cc in range(4):
                    cp, cpb = cps[cc]
                    for k in range(31):
                        di = dgc["i"]
                        dgc["i"] = (di + 1) % NDG
                        dsl = dg[:, di * 128:(di + 1) * 128]
                        A_("pool", (lambda o, w: lambda e: e.tensor_scalar(out=o, in0=ident[:, :], scalar1=w, scalar2=0.0, op0=ALU.mult, op1=ALU.add))(
                            dsl, convw[:, cc * 31 + k: cc * 31 + k + 1]), (LB, CONSTB), (DGB[di],))
                        src = G[:, cc * GW + gu0 - 15 + k: cc * GW + gu0 - 15 + k + ncol]
                        mmk(cp[:, c0:c0 + ncol], dsl, src, k == 0, k == 30, (DGB[di], GB), (cpb,))
                for cc in range(4):
                    cp, cpb = cps[cc]
                    A_("act", (lambda o, a, b: lambda e: e.activation(out=o, in_=a, func=AF.Identity, bias=b, scale=1.0))(cgb[:, cc * 512:cc * 512 + n], cp[:, 0:n], convv[:, cc:cc + 1]), (cpb, LB), (CGB,))
                    A_("act", (lambda o, a, b: lambda e: e.activation(out=o, in_=a, func=AF.Square, bias=b, scale=1.0))(cgq[:, cc * 512:cc * 512 + n], cp[:, 0:n], convv[:, cc:cc + 1]), (cpb, LB), (K["sqB"][cc],))
                s1, s1b = sbank()
                for cc in range(4):
                    mmk(s1[:, 0:n], ones[:, :], cgb[:, cc * 512:cc * 512 + n], cc == 0, cc == 3, (CGB, ONESB), (s1b,))
                s2, s2b = sbank()
                for cc in range(4):
                    mmk(s2[:, 0:n], ones[:, :], cgq[:, cc * 512:cc * 512 + n], cc == 0, cc == 3, (K["sqB"][cc], ONESB), (s2b,))
                A_("dve", lambda e: e.tensor_scalar(out=mu[:, 0:n], in0=s1[:, 0:n], scalar1=1.0 / 512, scalar2=None, op0=ALU.mult), (s1b,), (STB,))
                A_("dve", lambda e: e.tensor_tensor(out=var[:, 0:n], in0=mu[:, 0:n], in1=mu[:, 0:n], op=ALU.mult), (STB,), (STB,))
                A_("dve", lambda e: e.scalar_tensor_tensor(out=var[:, 0:n], in0=s2[:, 0:n], scalar=1.0 / 512, in1=var[:, 0:n], op0=ALU.mult, op1=ALU.subtract), (s2b, STB), (STB,))
                A_("act", lambda e: e.activation(out=rstd[:, 0:n], in_=var[:, 0:n], func=AF.Sqrt, bias=epsc[:, 0:1], scale=1.0), (STB, CONSTB), (STB,))
                A_("dve", lambda e: e.reciprocal(out=rstd[:, 0:n], in_=rstd[:, 0:n]), (STB,), (STB,))
                for cc in range(4):
                    cp, cpb = cps[cc]
                    q = cc % 2
                    lt = K["tmp"][:, q * 512:q * 512 + n]
                    A_("dve", (lambda o, a, b: lambda e: e.scalar_tensor_tensor(out=o, in0=a, scalar=b, in1=mu[:, 0:n], op0=ALU.add, op1=ALU.subtract))(lt, cp[:, 0:n], convv[:, cc:cc + 1]), (cpb, LB, STB), (K["tmpB"][q],))
                    A_("dve", (lambda o: lambda e: e.tensor_tensor(out=o, in0=o, in1=rstd[:, 0:n], op=ALU.mult))(lt), (K["tmpB"][q], STB), (K["tmpB"][q],))
                    A_("act", (lambda o, a, sc, b: lambda e: e.activation(out=o, in_=a, func=AF.Silu, bias=b, scale=sc))(h[:, cc, 0:n], lt, convv[:, 4 + cc:5 + cc], convv[:, 8 + cc:9 + cc]), (K["tmpB"][q], LB), (HB,))
                    A_("act", (lambda o, a: lambda e: e.activation(out=o, in_=a, func=AF.Copy))(h[:, 4 + cc, 0:n], BO[:, cc * 640 + bo0: cc * 640 + bo0 + n]), (BOB,), (HB,))
                outproj_post(l, r, 1, w_out[j], tokg0, n, K, last=last)

            if "grp" not in mix:
                group = lambda *a, **k: None
            if ph == "P":
                for t0 in range(0, NT, 512):
                    inproj(t0, 512, [(0, 256, 16), (256, 256, 304)], 0)
                    group(t0, 256, [(0, 256, 16)], 0)
                    group(t0 + 256, 256, [(0, 256, 304)], 256, last=(t0 + 512 >= NT))
            else:
                for t0 in range(0, NT, 512):
                    n = min(512, NT - t0)
                    if t0 > 0:
                        for cc in range(4):
                            A_("act", (lambda o, a: lambda e: e.activation(out=o, in_=a, func=AF.Copy))(G[:, cc * GW: cc * GW + 144], G[:, cc * GW + 512: cc * GW + 656]), (GB,), (GB,))
                            A_("act", (lambda o, a: lambda e: e.activation(out=o, in_=a, func=AF.Copy))(BO[:, cc * 640: cc * 640 + 128], BO[:, cc * 640 + 512: cc * 640 + 640]), (BOB,), (BOB,))
                    inproj(t0, n, [(0, n, 144)], 128)
                    if t0 == 0:
                        group(0, n - 128, [(0, n - 128, 144)], 128)
                    else:
                        group(t0 - 128, n, [(0, n, 16)], 0, last=(t0 + 512 >= NT and not flush))
                if flush:
                    assert NT % 512 == 0
                    group(NT - 128, 128, [(0, 128, 528)], 512)

        for ph in phases:
            NT = NTP if ph == "P" else NTS
            r = 0 if ph == "P" else 1
            for c in range(8):
                P.add("sp", (lambda o, a: lambda e: e.dma_start(out=o, in_=a))(x[:, c, 0:NT], xT[ph][c * 128:(c + 1) * 128, :]),
                      (), XB[0:NT // 128], dma=True)
            def mk_tiles(ntok):
                return [(t0, min(512, ntok - t0)) for t0 in range(0, ntok, 512)]
            for l in range(depth):
                shrink = (ph == "S") and SHRINK and depth == DEPTH
                nt1 = NT - 128 * l if shrink else NT
                nt2 = nt1 - 128 if shrink else NT
                K = ffn_carve()
                prepre["next"] = (l, 1, min(512, nt1)) if (do_mixer and PREPRE) else None
                ffn_sublayer(l, 0, r, mk_tiles(nt1), K)
                P.barrier()
                if do_mixer:
                    prepre["mixnext"] = (l, 2, min(512, nt2)) if PREPRE else None
                    if l % 2 == 1:
                        if "attn" in mix:
                            attn_layer(l, ph, r, nt1, flush=not shrink)
                    else:
                        if "conv" in mix:
                            conv_layer(l, ph, r, nt1, flush=not shrink)
                    P.barrier()
                K = ffn_carve()
                nxt1 = (NT - 128 * (l + 1)) if shrink else NT
                prepre["next"] = (l + 1, 0, min(512, nxt1)) if (l + 1 < depth and PREPRE) else None
                ffn_sublayer(l, 1, r, mk_tiles(nt2), K)
                P.barrier()
            NO = NTP if ph == "P" else NOWN
            for c in range(8):
                P.add("sp", (lambda o, a: lambda e: e.dma_start(out=o, in_=a))(yT[ph][c * 128:(c + 1) * 128, :], x[:, c, 0:NO]),
                      XB[0:NO // 128], (), dma=True)
            P.barrier()

        cnt = P.finalize()
        print("ops", len(P.ops), "signals", cnt, "dmas", P.ndma)
        with nc.Block() as block:
            @block.tensor
            def _(e):
                P.emit("pe", e, esem, dsem)

            @block.scalar
            def _(e):
                P.emit("act", e, esem, dsem)

            @block.vector
            def _(e):
                P.emit("dve", e, esem, dsem)

            @block.gpsimd
            def _(e):
                P.emit("pool", e, esem, dsem)

            @block.sync
            def _(e):
                P.emit("sp", e, esem, dsem)
    return nc


_PERM = np.concatenate([np.arange(16, 32), np.arange(0, 16), np.arange(48, 64), np.arange(32, 48)])


def _fm(v, nch):
    return np.ascontiguousarray(np.asarray(v, np.float32).reshape(nch, 128).T)


def _prep_shared(inp):
    sh = {}
    for k in ("w_mod", "ffn_w_gate", "ffn_w_up", "ffn_w_down", "cm_w_in", "cm_w_out", "attn_w_o"):
        sh[k] = np.ascontiguousarray(inp[k], dtype=np.float32)
    wq = np.asarray(inp["attn_w_qkv"], np.float32)
    blocks = []
    for j in range(2):
        q = wq[j][:, :1024]
        k = wq[j][:, 1024:1280]
        qperm = q.reshape(1024, 16, 64)[:, :, _PERM].reshape(1024, 1024)
        k4 = k.reshape(1024, 4, 64)
        kdup = np.concatenate([k4, k4], axis=2).reshape(1024, 512)
        kp4 = k4[:, :, _PERM]
        kpdup = np.concatenate([kp4, kp4], axis=2).reshape(1024, 512)
        blocks.append(np.concatenate([q, qperm, kdup, kpdup, wq[j][:, 1024:1536]], axis=1))
    sh["w_att"] = np.ascontiguousarray(np.stack(blocks))
    bm = np.asarray(inp["b_mod"], np.float32)
    sh["bmodT"] = np.ascontiguousarray(np.concatenate([_fm(bm[l], 72) for l in range(DEPTH)], axis=1))
    nw = np.asarray(inp["norm_w"], np.float32)
    sh["normwT"] = np.ascontiguousarray(np.concatenate([_fm(nw[l].reshape(-1), 48) for l in range(DEPTH)], axis=1))
    cv = []
    sl = []
    for j in range(2):
        cv.append(np.concatenate([_fm(inp[k][j], 4) for k in ("cm_conv_b", "cm_conv_ln_g", "cm_conv_ln_b")], axis=1))
        sl.append(np.concatenate([np.tile(np.asarray(inp[k][j], np.float32)[None, :], (128, 1))
                                  for k in ("cm_sgu_ln_g", "cm_sgu_ln_b")], axis=1))
    sh["convv"] = np.ascontiguousarray(np.stack(cv))
    sh["slnv"] = np.ascontiguousarray(np.stack(sl))
    sk = np.asarray(inp["attn_sink"], np.float32)
    sk = sk.reshape(2, 4, 4)[:, :, [0, 2, 1, 3]].reshape(2, 16)
    sh["sinkrow"] = np.ascontiguousarray(np.tile(np.repeat(sk, 128, axis=1).reshape(2, 1, 2048), (1, 128, 1)))
    a = np.arange(128)
    mprev = (a[None, :] <= a[:, None]).astype(np.float32)
    mnext = (a[:, None] <= a[None, :]).astype(np.float32)
    sh["ident"] = np.ascontiguousarray(np.eye(128, dtype=np.float32))
    sh["masks"] = np.ascontiguousarray(np.concatenate([np.tile(mprev, (1, 4)), np.tile(mnext, (1, 4))], axis=1))
    return sh


def _conv_sgu(inp, rev):
    cw = np.asarray(inp["cm_conv_w"], np.float32)
    sw = np.asarray(inp["cm_sgu_w"], np.float32)
    sb_ = np.asarray(inp["cm_sgu_b"], np.float32)
    if rev:
        cw = cw[:, ::-1, :]
        sw = sw[:, :, ::-1, ::-1]
        sb_ = sb_[:, :, ::-1]
    convw = np.stack([np.ascontiguousarray(cw[j].T.reshape(4, 128, 31).transpose(1, 0, 2)).reshape(128, 124) for j in range(2)])
    sguw = np.stack([np.ascontiguousarray(sw[j].transpose(2, 0, 1)).reshape(128, 512) for j in range(2)])
    sgub = np.ascontiguousarray(np.tile(np.tile(sb_.reshape(2, 4, 1, 128), (1, 1, 4, 1)).reshape(2, 1, 2048), (1, 128, 1)))
    return np.ascontiguousarray(convw), np.ascontiguousarray(sguw), sgub


def _rope_tables(rev):
    t = np.arange(NTS)
    g = (4095 - t) if rev else t
    row = (g // 64).astype(np.float32)
    col = (g % 64).astype(np.float32)
    invf = np.power(np.float32(10000.0), -np.arange(0, 32, 2, dtype=np.float32) / np.float32(32)).astype(np.float32)
    cs = np.zeros((128, 2, NTS), np.float32)
    for p in range(128):
        d = p % 64
        if d < 32:
            ang = row * invf[d % 16]
            sign = -1.0 if d < 16 else 1.0
        else:
            ang = col * invf[(d - 32) % 16]
            sign = -1.0 if (d - 32) < 16 else 1.0
        ang = ang.astype(np.float32)
        cs[p, 0] = np.cos(ang)
        cs[p, 1] = sign * np.sin(ang)
    return np.ascontiguousarray(cs.reshape(128, 2 * NTS))


def _prep_core(inp, sh, i, cache):
    b, half = i // 2, i % 2
    m = dict(sh)
    xp = np.asarray(inp["x_prompt"], np.float32)[4 * i:4 * i + 4].reshape(NTP, D)
    m["xpT"] = np.ascontiguousarray(xp.T)
    xs = np.asarray(inp["x_sample"], np.float32)[b]
    xs = xs[::-1][:NTS] if half else xs[:NTS]
    m["xsT"] = np.ascontiguousarray(xs.T)
    cond = np.stack([_fm(inp["c_ctx"], 8), _fm(inp["c"][b], 8)], axis=2).reshape(128, 16)
    m["condT"] = np.ascontiguousarray(cond)
    for rev in (0, 1):
        if ("cs", rev) not in cache:
            cache[("cs", rev)] = _conv_sgu(inp, rev), _rope_tables(rev)
    (cw_p, sw_p, sb_p), _ = cache[("cs", 0)]
    (cw_s, sw_s, sb_s), cs = cache[("cs", half)]
    m.update(convw_p=cw_p, sguw_p=sw_p, sgub_p=sb_p, convw_s=cw_s, sguw_s=sw_s, sgub_s=sb_s, cs=cs)
    ck = np.asarray(inp["cache_k"], np.float32)[b]
    cvv = np.asarray(inp["cache_v"], np.float32)[b]
    kt = ck.transpose(0, 3, 2, 1)
    m["kcT"] = np.ascontiguousarray(np.concatenate([kt, kt], axis=1).reshape(2, 128, 1024))
    m["vc"] = np.ascontiguousarray(cvv.reshape(2, 2, 128, 256).transpose(0, 2, 1, 3).reshape(2, 128, 512))
    return m


_NC_CACHE = {}


def kernel(**inputs):
    key = "full"
    if key not in _NC_CACHE:
        _NC_CACHE[key] = build()
    nc = _NC_CACHE[key]
    sh = _prep_shared(inputs)
    cache = {}
    in_maps = [_prep_core(inputs, sh, i, cache) for i in range(8)]
    res = run_bass_kernel_spmd(nc, in_maps, core_ids=list(range(8)))
    yp = np.zeros((32, 256, D), np.float32)
    ys = np.zeros((4, 4096, D), np.float32)
    nk = np.zeros((32, 2, 256, 4, 64), np.float32)
    nv = np.zeros((32, 2, 256, 4, 64), np.float32)
    for i in range(8):
        r = res.results[i]
        b, half = i // 2, i % 2
        yp[4 * i:4 * i + 4] = np.asarray(r["ypT"]).T.reshape(4, 256, D)
        o = np.asarray(r["ysT"]).T
        if half:
            ys[b, 2048:] = o[::-1]
        else:
            ys[b, :2048] = o
        nk[4 * i:4 * i + 4] = np.asarray(r["nk"]).reshape(2, 4, 256, 4, 64).transpose(1, 0, 2, 3, 4)
        nv[4 * i:4 * i + 4] = np.asarray(r["nv"]).reshape(2, 4, 256, 4, 64).transpose(1, 0, 2, 3, 4)
    return yp, ys, nk, nv
```

```python
import os
import numpy as np
from contextlib import ExitStack
import concourse.bass as bass
import concourse.mybir as mybir
from concourse.bass_utils import run_bass_kernel_spmd

F32 = mybir.dt.float32
BF16 = mybir.dt.bfloat16
AF = mybir.ActivationFunctionType
ALU = mybir.AluOpType

D = 1024
DFF = 2816
NFC = 22
DEPTH = 4
NTP = 1024
NTS = 2560
NOWN = 2048
EPS = 1e-6
KQ = 8
NWA = 6
NWD = 2
SCRW = 19200
USE_GELU_TANH = False
SHRINK = True
PREPRE = True


class Buf:
    __slots__ = ("name", "w", "r")

    def __init__(self, name):
        self.name = name
        self.w = None
        self.r = []


class Op:
    __slots__ = ("eng", "fn", "deps", "dma", "sig", "sigval", "dn", "idx", "waits")


class Prog:
    def __init__(self):
        self.ops = []
        self.ndma = {"sp": 0, "pool": 0}
        self.dmaops = {"sp": [], "pool": []}
        self.last = {}
        self.pend = {}

    def add(self, eng, fn, reads=(), writes=(), dma=False, exempt=False):
        op = Op()
        op.eng = eng
        op.fn = fn
        op.dma = dma
        op.sig = False
        op.sigval = 0
        op.idx = len(self.ops)
        deps = set()
        for b in reads:
            if b.w is not None:
                deps.add(b.w)
        for b in writes:
            if b.w is not None:
                deps.add(b.w)
            for r in b.r:
                deps.add(r)
        for b in writes:
            b.w = op
            b.r = []
        for b in reads:
            if not dma:
                b.r = [o for o in b.r if o.dma or o.eng != eng]
            b.r.append(op)
        if dma:
            n = self.ndma[eng]
            self.ndma[eng] += 1
            op.dn = n
            if n >= KQ:
                deps.add(self.dmaops[eng][n - KQ])
            self.dmaops[eng].append(op)
        if eng in self.pend and not exempt:
            deps |= self.pend.pop(eng)
        deps.discard(op)
        op.deps = deps
        self.ops.append(op)
        if not dma:
            self.last[eng] = op
        return op

    def barrier(self):
        src = set(self.last.values())
        for q in ("sp", "pool"):
            src |= set(self.dmaops[q][-KQ:])
        for e in ("act", "dve", "pool", "sp"):
            self.pend[e] = set(src) | self.pend.get(e, set())

    def finalize(self):
        src = set()
        for q in ("sp", "pool"):
            src |= set(self.dmaops[q][-KQ:])
        op = self.add("sp", None)
        op.deps |= src
        seen = {}
        for op in self.ops:
            ws = []
            for d in sorted(op.deps, key=lambda o: o.idx):
                if d.dma:
                    key = ("dma", d.eng, d.dn % KQ)
                    val = d.dn // KQ + 1
                else:
                    if d.eng == op.eng and not op.dma and op.eng == "pe":
                        continue
                    key = ("eng", d.eng)
                    val = d.idx
                sk = (op.eng, key)
                if seen.get(sk, -1) >= val:
                    continue
                seen[sk] = val
                ws.append(d)
                if not d.dma:
                    d.sig = True
            op.waits = ws
        cnt = {}
        for op in self.ops:
            if op.sig:
                cnt[op.eng] = cnt.get(op.eng, 0) + 1
                op.sigval = cnt[op.eng]
        return cnt

    def emit(self, engname, eng, esem, dsem):
        for op in self.ops:
            if op.eng != engname:
                continue
            for d in op.waits:
                if d.dma:
                    eng.wait_ge(dsem[d.eng][d.dn % KQ], 16 * (d.dn // KQ + 1))
                else:
                    eng.wait_ge(esem[d.eng], d.sigval)
            if op.fn is None:
                continue
            ins = op.fn(eng)
            if op.dma:
                ins.then_inc(dsem[op.eng][op.dn % KQ], 16)
            elif op.sig:
                ins.then_inc(esem[op.eng], 1)


def build(depth=DEPTH, phases=("P", "S"), do_mixer=True, dbg=False, mix=("conv", "grp", "attn", "att", "op")):
    nc = bass.Bass("TRN2", target_bir_lowering=False)
    P = Prog()

    def din(name, shape):
        return nc.dram_tensor(name, list(shape), F32, kind="ExternalInput").ap()

    def dout(name, shape):
        return nc.dram_tensor(name, list(shape), F32, kind="ExternalOutput").ap()

    xT = {"P": din("xpT", (D, NTP)), "S": din("xsT", (D, NTS))}
    yT = {"P": dout("ypT", (D, NTP)), "S": dout("ysT", (D, NOWN))}
    nk_o = dout("nk", (2, 4, 256, 256))
    nv_o = dout("nv", (2, 4, 256, 256))
    w_mod = din("w_mod", (DEPTH, D, 9 * D))
    w_gate = din("ffn_w_gate", (DEPTH, 2, D, DFF))
    w_up = din("ffn_w_up", (DEPTH, 2, D, DFF))
    w_down = din("ffn_w_down", (DEPTH, 2, DFF, D))
    w_in = din("cm_w_in", (2, D, 2048))
    w_out = din("cm_w_out", (2, D, D))
    w_att = din("w_att", (2, D, 3584))
    w_o = din("attn_w_o", (2, D, D))
    bmodT_d = din("bmodT", (128, 288))
    normwT_d = din("normwT", (128, 192))
    cond_d = din("condT", (128, 16))
    convw_d = {"P": din("convw_p", (2, 128, 124)), "S": din("convw_s", (2, 128, 124))}
    convv_d = din("convv", (2, 128, 12))
    slnv_d = din("slnv", (2, 128, 1024))
    sguw_d = {"P": din("sguw_p", (2, 128, 512)), "S": din("sguw_s", (2, 128, 512))}
    sgub_d = {"P": din("sgub_p", (2, 128, 2048)), "S": din("sgub_s", (2, 128, 2048))}
    sink_d = din("sinkrow", (2, 128, 2048))
    kcT_d = din("kcT", (2, 128, 1024))
    vc_d = din("vc", (2, 128, 512))
    cs_d = din("cs", (128, 2 * NTS))
    mask_d = din("masks", (128, 1024))
    ident_d = din("ident", (128, 128))

    dbg_o = dout("dbg", (128, 2048)) if dbg else None
    es = ExitStack()
    with es:
        def sb(name, shape, dt):
            return es.enter_context(nc.sbuf_tensor(name, list(shape), dt))

        x = sb("x", (128, 8, NTS), F32)
        XB = [Buf(f"x{i}") for i in range(NTS // 128)]
        scr = sb("scr", (128, SCRW if not dbg else SCRW - 2200), F32)
        h = sb("h", (128, 8, 512), BF16)
        HB = Buf("h")
        wa = sb("wa", (128, NWA, 8, 256), BF16)
        WAB = [Buf(f"wa{i}") for i in range(NWA)]
        wd = sb("wd", (128, NWD, NFC, 128), BF16)
        WDB = [Buf(f"wd{i}") for i in range(NWD)]
        ones = sb("ones", (128, 128), BF16)
        ONESB = Buf("ones")
        e0 = sb("e0", (128, 128), BF16)
        ident = sb("ident_s", (128, 128), BF16)
        bmodT = sb("bmodT_s", (128, 288), F32)
        normwT = sb("normwT_s", (128, 192), F32)
        condr = sb("condr", (128, 16), F32)
        condb = sb("condb", (128, 8, 2), BF16)
        modsb = sb("modsb", (128, DEPTH, 2, 72), F32)
        coefA = sb("coefA", (128, DEPTH, 2, 3, 8), F32)
        coefG = sb("coefG", (128, DEPTH, 2, 3, 8), F32)
        CONSTB = Buf("const")
        masks = sb("masks_s", (128, 1024), BF16)
        ps = es.enter_context(nc.psum_tensor("ps", [128, 8, 512], F32))
        PSB = [Buf(f"ps{i}") for i in range(8)]
        esem = {e: es.enter_context(nc.semaphore(f"se_{e}")) for e in ("pe", "act", "dve", "pool")}
        dsem = {q: [es.enter_context(nc.semaphore(f"sd_{q}{i}")) for i in range(KQ)] for q in ("sp", "pool")}

        st = {"bank": 0, "wa": 0, "wd": 0}

        def bank():
            i = st["bank"]
            st["bank"] = (i + 1) % 6
            return ps[:, i, :], PSB[i]

        def sbank():
            i = 6 + st.get("sbank", 0)
            st["sbank"] = (i - 6 + 1) % 2
            return ps[:, i, :], PSB[i]

        class Carver:
            def __init__(self):
                self.off = 0

            def f32(self, n, *dims):
                ap = scr[:, self.off:self.off + n]
                self.off += n
                assert self.off <= SCRW, self.off
                return ap

            def bf(self, n):
                assert n % 2 == 0
                ap = scr[:, self.off:self.off + n // 2].bitcast(BF16)
                self.off += n // 2
                assert self.off <= SCRW, self.off
                return ap

        def load_wa(src2d, col0, ncols):
            i = st["wa"]
            st["wa"] = (i + 1) % NWA
            dst = wa[:, i, :, 0:ncols]
            src = src2d.rearrange("(k p) n -> p k n", p=128)[:, :, col0:col0 + ncols]
            P.add("pool", lambda e: e.dma_start(out=dst, in_=src), (), (WAB[i],), dma=True, exempt=True)
            return wa[:, i], WAB[i]

        def load_wd(src2d, col0):
            i = st["wd"]
            st["wd"] = (i + 1) % NWD
            dst = wd[:, i, :, :]
            src = src2d.rearrange("(f p) n -> p f n", p=128)[:, :, col0:col0 + 128]
            P.add("pool", lambda e: e.dma_start(out=dst, in_=src), (), (WDB[i],), dma=True, exempt=True)
            return wd[:, i], WDB[i]

        P.add("dve", lambda e: e.memset(ones[:], 1.0), (), (ONESB,))
        P.add("dve", lambda e: e.memset(e0[:], 0.0), (), (ONESB,))
        P.add("dve", lambda e: e.memset(e0[0:1, :], 1.0), (), (ONESB,))
        P.add("sp", lambda e: e.dma_start(out=bmodT[:], in_=bmodT_d), (), (CONSTB,), dma=True)
        P.add("sp", lambda e: e.dma_start(out=normwT[:], in_=normwT_d), (), (CONSTB,), dma=True)
        P.add("sp", lambda e: e.dma_start(out=condr[:], in_=cond_d), (), (CONSTB,), dma=True)
        P.add("pool", lambda e: e.dma_start(out=masks[:], in_=mask_d), (), (CONSTB,), dma=True)
        P.add("pool", lambda e: e.dma_start(out=ident[:], in_=ident_d), (), (CONSTB,), dma=True)
        P.add("act", lambda e: e.activation(out=condb[:].rearrange("p k r -> p (k r)"), in_=condr[:], func=AF.Silu),
              (CONSTB,), (CONSTB,))
        for l in range(depth):
            mp, mpb = bank()
            mpv = mp[:, 0:144].rearrange("p (f r) -> p f r", r=2)
            for pc in range(36):
                wt, wb = load_wa(w_mod[l], pc * 256, 256)
                for fi in range(2):
                    fc = pc * 2 + fi
                    for k in range(8):
                        P.add("pe", (lambda o, a, b, k=k: lambda e: e.matmul(o, lhsT=a, rhs=b, start=(k == 0), stop=(k == 7)))(
                            mpv[:, fc, :], wt[:, k, fi * 128:(fi + 1) * 128], condb[:, k, :]),
                            (wb, CONSTB), (mpb,))
            for r in range(2):
                P.add("dve", (lambda o, a, b: lambda e: e.tensor_tensor(out=o, in0=a, in1=b, op=ALU.add))(
                    modsb[:, l, r, :], mpv[:, :, r], bmodT[:, l * 72:(l + 1) * 72]), (mpb, CONSTB), (CONSTB,))
                for s in range(3):
                    wgt = 1.0 if s == 1 else 0.5
                    P.add("dve", (lambda o, a, b: lambda e: e.scalar_tensor_tensor(out=o, in0=a, scalar=1.0, in1=b, op0=ALU.add, op1=ALU.mult))(
                        coefA[:, l, r, s, :], modsb[:, l, r, (3 * s + 1) * 8:(3 * s + 2) * 8],
                        normwT[:, l * 48 + 2 * s * 8: l * 48 + 2 * s * 8 + 8]), (CONSTB,), (CONSTB,))
                    P.add("dve", (lambda o, a, b, w: lambda e: e.scalar_tensor_tensor(out=o, in0=a, scalar=w, in1=b, op0=ALU.mult, op1=ALU.mult))(
                        coefG[:, l, r, s, :], modsb[:, l, r, (3 * s + 2) * 8:(3 * s + 3) * 8],
                        normwT[:, l * 48 + (2 * s + 1) * 8: l * 48 + (2 * s + 1) * 8 + 8], wgt), (CONSTB,), (CONSTB,))
        P.barrier()

        def xbufs(tok0, n):
            return [XB[b] for b in range(tok0 // 128, (tok0 + n + 127) // 128)]

        def rstd_from(ssb, ssp, n, rs_ap, rsB, ndim):
            P.add("act", lambda e: e.activation(out=rs_ap[:, 0:n], in_=ssp[:, 0:n], func=AF.Sqrt, bias=epsc[:, 0:1], scale=1.0 / ndim),
                  (ssb, CONSTB), (rsB,))
            P.add("dve", lambda e: e.reciprocal(out=rs_ap[:, 0:n], in_=rs_ap[:, 0:n]), (rsB,), (rsB,))

        prepre = {"k": None, "next": None}

        def pre(l, r, s, tok0, n, K, stage="ab"):
            if stage == "ab" and prepre.get("k") == (l, s, tok0, n):
                prepre["k"] = None
                return
            xb = xbufs(tok0, n)
            rs_, rsB_ = K.get("rs2", K["rs"]), K.get("rs2B", K["rsB"])
            sq_, sqB_ = K.get("sqp", K["sq"]), K.get("sqpB", K["sqB"])

            def square(c):
                q = c % 4
                P.add("act", (lambda o, a: lambda e: e.activation(out=o, in_=a, func=AF.Square))(
                    sq_[:, q * 512:q * 512 + n], x[:, c, tok0:tok0 + n]), xb, (sqB_[q],))

            if "a" in stage:
                for c in range(4):
                    square(c)
                if stage == "a":
                    return
            ssp, ssb = sbank()
            for c in range(8):
                q = c % 4
                if c >= 4:
                    square(c)
                P.add("pe", (lambda o, b, c=c: lambda e: e.matmul(o, lhsT=ones[:], rhs=b, start=(c == 0), stop=(c == 7)))(
                    ssp[:, 0:n], sq_[:, q * 512:q * 512 + n]), (sqB_[q], ONESB), (ssb,))
            rstd_from(ssb, ssp, n, rs_, rsB_, D)
            for c in range(8):
                q = c % 2
                P.add("dve", (lambda o, a, sc, b: lambda e: e.scalar_tensor_tensor(out=o, in0=a, scalar=sc, in1=b, op0=ALU.mult, op1=ALU.mult))(
                    K["tmp"][:, q * 512:q * 512 + n], x[:, c, tok0:tok0 + n], coefA[:, l, r, s, c:c + 1], rs_[:, 0:n]),
                    xb + [rsB_, CONSTB], (K["tmpB"][q],))
                P.add("act", (lambda o, a, b: lambda e: e.activation(out=o, in_=a, func=AF.Identity, bias=b, scale=1.0))(
                    h[:, c, 0:n], K["tmp"][:, q * 512:q * 512 + n], modsb[:, l, r, 3 * s * 8 + c: 3 * s * 8 + c + 1]),
                    (K["tmpB"][q], CONSTB), (HB,))

        def post(l, r, s, tok0, n, K):
            xb = xbufs(tok0, n)
            rstd_from(K["ssb"], K["ssp"], n, K["rs"], K["rsB"], D)
            for c in range(8):
                q = c % 2
                P.add("dve", (lambda o, a, sc, b: lambda e: e.scalar_tensor_tensor(out=o, in0=a, scalar=sc, in1=b, op0=ALU.mult, op1=ALU.mult))(
                    K["tmp"][:, q * 512:q * 512 + n], K["ys"][:, c * 512:c * 512 + n], coefG[:, l, r, s, c:c + 1], K["rs"][:, 0:n]),
                    (K["ysB"], K["rsB"], CONSTB), (K["tmpB"][q],))
                P.add("dve", (lambda o, a, b: lambda e: e.tensor_tensor(out=o, in0=a, in1=b, op=ALU.add))(
                    x[:, c, tok0:tok0 + n], x[:, c, tok0:tok0 + n], K["tmp"][:, q * 512:q * 512 + n]),
                    xb + [K["tmpB"][q]], xb)

        def evac_y(K, yp, ypb, d, n, pend):
            q = d % 4
            P.add("act", (lambda o, a: lambda e: e.activation(out=o, in_=a, func=AF.Copy))(
                K["ys"][:, d * 512:d * 512 + n], yp[:, 0:n]), (ypb,), (K["ysB"],))
            P.add("act", (lambda o, a: lambda e: e.activation(out=o, in_=a, func=AF.Square))(
                K["sq"][:, q * 512:q * 512 + n], yp[:, 0:n]), (ypb,), (K["sqB"][q],))
            pend.append((d, q))

        def flush_ss(K, n, pend):
            while pend:
                d, q = pend.pop(0)
                P.add("pe", (lambda o, b, d=d: lambda e: e.matmul(o, lhsT=ones[:], rhs=b, start=(d == 0), stop=(d == 7)))(
                    K["ssp"][:, 0:n], K["sq"][:, q * 512:q * 512 + n]), (K["sqB"][q], ONESB), (K["ssb"],))

        def outproj_post(l, r, s, wsrc, tok0, n, K):
            K["ssp"], K["ssb"] = sbank()
            pend = []
            for pc in range(4):
                wt, wb = load_wa(wsrc, pc * 256, 256)
                for di in range(2):
                    d = pc * 2 + di
                    yp, ypb = bank()
                    for k in range(8):
                        P.add("pe", (lambda o, a, b, k=k: lambda e: e.matmul(o, lhsT=a, rhs=b, start=(k == 0), stop=(k == 7)))(
                            yp[:, 0:n], wt[:, k, di * 128:(di + 1) * 128], h[:, k, 0:n]), (wb, HB), (ypb,))
                    flush_ss(K, n, pend)
                    evac_y(K, yp, ypb, d, n, pend)
            flush_ss(K, n, pend)
            post(l, r, s, tok0, n, K)

        epsc = sb("epsc", (128, 1), F32)
        P.add("dve", lambda e: e.memset(epsc[:], EPS), (), (CONSTB,))

        def ffn_carve():
            C = Carver()
            K = {}
            K["a"] = C.bf(NFC * 512)
            K["aB"] = [Buf(f"a{f}") for f in range(NFC)]
            K["ys"] = C.f32(8 * 512)
            K["ysB"] = Buf("ys")
            K["tmp"] = C.f32(2 * 512)
            K["tmpB"] = [Buf("tmp0"), Buf("tmp1")]
            K["sq"] = C.bf(4 * 512)
            K["sqB"] = [Buf(f"sq{i}") for i in range(4)]
            K["sg"] = C.f32(2 * 512)
            K["sgB"] = [Buf("sg0"), Buf("sg1")]
            K["rs"] = C.f32(512)
            K["rsB"] = Buf("rs")
            K["rs2"] = C.f32(512)
            K["rs2B"] = Buf("rs2")
            K["sqp"] = C.bf(4 * 512)
            K["sqpB"] = [Buf(f"sqp{i}") for i in range(4)]
            return K

        dbgt = sb("dbgt", (128, 2048), F32) if dbg else None
        DBGB = Buf("dbg")
        dstate = {"done": False}

        def ffn_gateup(l, j, r, tok0, n, K):
            for pc in range(11):
                wg, wgb = load_wa(w_gate[l, j], pc * 256, 256)
                wu, wub = load_wa(w_up[l, j], pc * 256, 256)
                for fi in range(2):
                    f = pc * 2 + fi
                    gp, gpb = bank()
                    up, upb = bank()
                    for k in range(8):
                        P.add("pe", (lambda o, a, b, k=k: lambda e: e.matmul(o, lhsT=a, rhs=b, start=(k == 0), stop=(k == 7)))(
                            gp[:, 0:n], wg[:, k, fi * 128:(fi + 1) * 128], h[:, k, 0:n]), (wgb, HB), (gpb,))
                    for k in range(8):
                        P.add("pe", (lambda o, a, b, k=k: lambda e: e.matmul(o, lhsT=a, rhs=b, start=(k == 0), stop=(k == 7)))(
                            up[:, 0:n], wu[:, k, fi * 128:(fi + 1) * 128], h[:, k, 0:n]), (wub, HB), (upb,))
                    q = f % 2
                    P.add("act", (lambda o, a: lambda e: e.activation(out=o, in_=a, func=AF.Silu))(
                        K["sg"][:, q * 512:q * 512 + n], gp[:, 0:n]), (gpb,), (K["sgB"][q],))
                    P.add("dve", (lambda o, a, b: lambda e: e.tensor_tensor(out=o, in0=a, in1=b, op=ALU.mult))(
                        K["a"][:, f * 512:f * 512 + n], up[:, 0:n], K["sg"][:, q * 512:q * 512 + n]),
                        (upb, K["sgB"][q]), (K["aB"][f],))

        def ffn_down(l, j, r, tok0, n, K, mid_hook=None):
            s = 0 if j == 0 else 2
            K["ssp"], K["ssb"] = sbank()
            pend = []
            for d in range(8):
                wt, wb = load_wd(w_down[l, j], d * 128)
                yp, ypb = bank()
                for f in range(NFC):
                    P.add("pe", (lambda o, a, b, f=f: lambda e: e.matmul(o, lhsT=a, rhs=b, start=(f == 0), stop=(f == NFC - 1)))(
                        yp[:, 0:n], wt[:, f, :], K["a"][:, f * 512:f * 512 + n]), (wb, K["aB"][f]), (ypb,))
                flush_ss(K, n, pend)
                evac_y(K, yp, ypb, d, n, pend)
                if d == 1 and mid_hook is not None:
                    mid_hook()
            flush_ss(K, n, pend)
            post(l, r, s, tok0, n, K)

        def ffn_sublayer(l, j, r, tiles, K):
            s = 0 if j == 0 else 2
            pre(l, r, s, tiles[0][0], tiles[0][1], K)
            for ti, (t0, n) in enumerate(tiles):
                ffn_gateup(l, j, r, t0, n, K)
                hook = None
                if ti + 1 < len(tiles):
                    nt0, nn = tiles[ti + 1]
                    pre(l, r, s, nt0, nn, K, stage="a")
                    hook = (lambda nt0=nt0, nn=nn: pre(l, r, s, nt0, nn, K, stage="b"))
                elif prepre["next"] is not None and len(tiles) > 1:
                    (l2, s2, n2) = prepre["next"]
                    pre(l2, r, s2, 0, n2, K, stage="a")

                    def hook(l2=l2, s2=s2, n2=n2):
                        pre(l2, r, s2, 0, n2, K, stage="b")
                        prepre["k"] = (l2, s2, 0, n2)
                ffn_down(l, j, r, t0, n, K, mid_hook=hook)

        def A_(eng, fn, R=(), W=(), dma=False):
            return P.add(eng, fn, tuple(R), tuple(W), dma=dma)

        def mmk(o, a, b, first, last, R, W):
            A_("pe", lambda e: e.matmul(o, lhsT=a, rhs=b, start=first, stop=last), R, W)

        def proj_fm(wt, wb, c0, n, out_ps, out_b):
            for k in range(8):
                mmk(out_ps[:, 0:n], wt[:, k, c0:c0 + 128], h[:, k, 0:n], k == 0, k == 7, (wb, HB), (out_b,))

        def attn_layer(l, ph, r, NT, flush=True):
            j = l // 2
            NB = NT // 128
            C = Carver()
            K = {}
            kT = C.bf(4 * 1024); KB = [Buf(f"k{i}") for i in range(8)]
            vt = C.bf(8 * 256); VB = [Buf(f"v{i}") for i in range(8)]
            qT = C.bf(8 * 768); QB = [Buf(f"q{i}") for i in range(6)]
            cs = C.bf(2 * NTS); CSB = Buf("cs")
            NPT = 5
            pT = C.bf(NPT * 512); PTB = [Buf(f"pt{i}") for i in range(NPT)]
            K["sq"] = C.bf(4 * 512); K["sqB"] = [Buf(f"sq{i}") for i in range(4)]
            K["tmp"] = C.f32(2 * 512); K["tmpB"] = [Buf("tmp0"), Buf("tmp1")]
            K["rs"] = C.f32(512); K["rsB"] = Buf("rs")
            K["ys"] = C.f32(8 * 512); K["ysB"] = Buf("ys")
            kc = C.bf(4 * 256); vcx = C.bf(2 * 256); esk = C.bf(2048); LB = Buf("lconst")
            rden = C.f32(512); RDB = Buf("rden")
            ptc = {"i": 0}
            A_("pool", lambda e: e.dma_start(out=kc, in_=kcT_d[j]), (), (LB,), dma=True)
            A_("pool", lambda e: e.dma_start(out=vcx, in_=vc_d[j]), (), (LB,), dma=True)
            A_("pool", lambda e: e.dma_start(out=esk, in_=sink_d[j]), (), (LB,), dma=True)
            A_("act", lambda e: e.activation(out=esk, in_=esk, func=AF.Exp), (LB,), (LB,))
            if ph == "S":
                A_("pool", lambda e: e.dma_start(out=cs, in_=cs_d), (), (CSB,), dma=True)

            def rope_or_copy(src_ps, srcb, perm_ps, permb, tok0, n, dsts):
                if ph == "P":
                    for (dst, db, c0, ncol) in dsts:
                        A_("act", (lambda o, a: lambda e: e.activation(out=o, in_=a, func=AF.Copy))(dst, src_ps[:, c0:c0 + ncol]), (srcb,), (db,))
                    return
                t1 = K["tmp"][:, 0:n]
                t2 = K["tmp"][:, 512:512 + n]
                A_("dve", lambda e: e.tensor_tensor(out=t1, in0=src_ps[:, 0:n], in1=cs[:, tok0:tok0 + n], op=ALU.mult), (srcb, CSB), (K["tmpB"][0],))
                A_("dve", lambda e: e.tensor_tensor(out=t2, in0=perm_ps[:, 0:n], in1=cs[:, NTS + tok0:NTS + tok0 + n], op=ALU.mult), (permb, CSB), (K["tmpB"][1],))
                for (dst, db, c0, ncol) in dsts:
                    A_("dve", (lambda o, a, b: lambda e: e.tensor_tensor(out=o, in0=a, in1=b, op=ALU.add))(
                        dst, K["tmp"][:, c0:c0 + ncol], K["tmp"][:, 512 + c0:512 + c0 + ncol]), K["tmpB"], (db,))

            STAGE = int(os.environ.get("ATT_STAGE", "99"))

            def project(tok0, n):
                b0 = tok0 // 128
                nb = n // 128
                if STAGE < 1:
                    return
                pre(l, r, 1, tok0, n, K)
                for pc in range(4 if STAGE >= 2 else 0):
                    wq, wqb = load_wa(w_att[j], pc * 256, 256)
                    if ph == "S":
                        wp, wpb = load_wa(w_att[j], 1024 + pc * 256, 256)
                    for ci in range(2):
                        c = pc * 2 + ci
                        qp, qpb = bank()
                        proj_fm(wq, wqb, ci * 128, n, qp, qpb)
                        pp, ppb = (None, None)
                        if ph == "S":
                            pp, ppb = bank()
                            proj_fm(wp, wpb, ci * 128, n, pp, ppb)
                        dsts = [(qT[:, c * 768 + ((b0 + bi) % 6) * 128: c * 768 + ((b0 + bi) % 6) * 128 + 128], QB[(b0 + bi) % 6], bi * 128, 128)
                                for bi in range(nb)]
                        rope_or_copy(qp, qpb, pp, ppb, tok0, n, dsts)
                ks0 = (b0 % 8)
                for pc in range(2 if STAGE >= 3 else 0):
                    wk, wkb = load_wa(w_att[j], 2048 + pc * 256, 256)
                    if ph == "S":
                        wp, wpb = load_wa(w_att[j], 2560 + pc * 256, 256)
                    for ci in range(2):
                        kv = pc * 2 + ci
                        kp, kpb = bank()
                        proj_fm(wk, wkb, ci * 128, n, kp, kpb)
                        pp, ppb = (None, None)
                        if ph == "S":
                            pp, ppb = bank()
                            proj_fm(wp, wpb, ci * 128, n, pp, ppb)
                        dsts = [(kT[:, kv * 1024 + ks0 * 128: kv * 1024 + ks0 * 128 + n], None, 0, n)]
                        if ph == "P":
                            A_("act", (lambda o, a: lambda e: e.activation(out=o, in_=a, func=AF.Copy))(dsts[0][0], kp[:, 0:n]), (kpb,), KB[ks0:ks0 + nb])
                        else:
                            t1 = K["tmp"][:, 0:n]
                            t2 = K["tmp"][:, 512:512 + n]
                            A_("dve", (lambda a: lambda e: e.tensor_tensor(out=t1, in0=a, in1=cs[:, tok0:tok0 + n], op=ALU.mult))(kp[:, 0:n]), (kpb, CSB), (K["tmpB"][0],))
                            A_("dve", (lambda a: lambda e: e.tensor_tensor(out=t2, in0=a, in1=cs[:, NTS + tok0:NTS + tok0 + n], op=ALU.mult))(pp[:, 0:n]), (ppb, CSB), (K["tmpB"][1],))
                            A_("dve", (lambda o: lambda e: e.tensor_tensor(out=o, in0=t1, in1=t2, op=ALU.add))(dsts[0][0]), K["tmpB"], KB[ks0:ks0 + nb])
                if STAGE < 4:
                    return
                if ph == "P":
                    wkk, wkkb = load_wa(w_att[j], 3072, 256)
                wvv, wvvb = load_wa(w_att[j], 3328, 256)
                for bi in range(nb):
                    blk = b0 + bi
                    vp, vpb = bank()
                    if ph == "P":
                        vp2, vpb2 = bank()
                        for k in range(8):
                            mmk(vp[:, 0:256], h[:, k, bi * 128:(bi + 1) * 128], wkk[:, k, :], k == 0, k == 7, (HB, wkkb), (vpb,))
                        for k in range(8):
                            mmk(vp2[:, 0:256], h[:, k, bi * 128:(bi + 1) * 128], wvv[:, k, :], k == 0, k == 7, (HB, wvvb), (vpb2,))
                        kvo = K["ys"][:, (bi % 2) * 512:(bi % 2) * 512 + 512]
                        A_("act", (lambda o, a: lambda e: e.activation(out=o, in_=a, func=AF.Copy))(kvo[:, 0:256], vp[:, 0:256]), (vpb,), (K["ysB"],))
                        A_("act", (lambda o, a: lambda e: e.activation(out=o, in_=a, func=AF.Copy))(kvo[:, 256:512], vp2[:, 0:256]), (vpb2,), (K["ysB"],))
                        if STAGE >= 5:
                            A_("sp", (lambda o, a: lambda e: e.dma_start(out=o, in_=a))(nk_o[j, blk // 2, (blk % 2) * 128:(blk % 2) * 128 + 128, :], kvo[:, 0:256]), (K["ysB"],), (), dma=True)
                            A_("sp", (lambda o, a: lambda e: e.dma_start(out=o, in_=a))(nv_o[j, blk // 2, (blk % 2) * 128:(blk % 2) * 128 + 128, :], kvo[:, 256:512]), (K["ysB"],), (), dma=True)
                        A_("dve", (lambda o, a: lambda e: e.tensor_copy(out=o, in_=a))(vt[:, (blk % 8) * 256:(blk % 8) * 256 + 256], kvo[:, 256:512]), (K["ysB"],), (VB[blk % 8],))
                    else:
                        for k in range(8):
                            mmk(vp[:, 0:256], h[:, k, bi * 128:(bi + 1) * 128], wvv[:, k, :], k == 0, k == 7, (HB, wvvb), (vpb,))
                        A_("act", (lambda o, a: lambda e: e.activation(out=o, in_=a, func=AF.Copy))(vt[:, (blk % 8) * 256:(blk % 8) * 256 + 256], vp[:, 0:256]), (vpb,), (VB[blk % 8],))

            AST = int(os.environ.get("ATT_A", "99"))

            def attend(i, keys, gi):
                qs = i % 6
                for kv in range(4):
                    pts = []
                    for (kind, kb, msk) in keys:
                        spA, spAb = bank()
                        spB, spBb = bank()
                        for hh in range(4):
                            c = 2 * kv + hh // 2
                            lo = (hh % 2) * 64
                            sp_, spb = (spA, spAb) if hh % 2 == 0 else (spB, spBb)
                            if kind == "l":
                                ksl = kT[lo:lo + 64, kv * 1024 + (kb % 8) * 128: kv * 1024 + (kb % 8) * 128 + 128]
                                kbuf = KB[kb % 8]
                            else:
                                ksl = kc[lo:lo + 64, kv * 256 + kb * 128: kv * 256 + kb * 128 + 128]
                                kbuf = LB
                            mmk(sp_[:, (hh // 2) * 128:(hh // 2 + 1) * 128], ksl, qT[lo:lo + 64, c * 768 + qs * 128: c * 768 + qs * 128 + 128], True, True,
                                (kbuf, QB[qs]), (spb,))
                        if AST < 2:
                            continue
                        pi = ptc["i"]
                        ptc["i"] = (pi + 1) % NPT
                        pt = pT[:, pi * 512:(pi + 1) * 512]
                        A_("act", (lambda o, a: lambda e: e.activation(out=o, in_=a, func=AF.Exp, scale=0.125))(pt[:, 0:256], spA[:, 0:256]), (spAb,), (PTB[pi],))
                        A_("act", (lambda o, a: lambda e: e.activation(out=o, in_=a, func=AF.Exp, scale=0.125))(pt[:, 256:512], spB[:, 0:256]), (spBb,), (PTB[pi],))
                        if msk:
                            mk = masks[:, (msk - 1) * 512: msk * 512]
                            A_("dve", (lambda o, m: lambda e: e.tensor_tensor(out=o, in0=o, in1=m, op=ALU.mult))(pt, mk), (PTB[pi], CONSTB), (PTB[pi],))
                        pts.append((pt, PTB[pi], kind, kb))
                    if AST < 3:
                        continue
                    rsp, rsb = sbank()
                    for n_, (pt, pb, kind, kb) in enumerate(pts):
                        mmk(rsp[:, :], ones[:, :], pt, n_ == 0, False, (pb, ONESB), (rsb,))
                    mmk(rsp[:, :], e0[:, :], esk[:, kv * 512:(kv + 1) * 512], False, True, (LB, ONESB), (rsb,))
                    if AST < 4:
                        continue
                    obanks = [bank(), bank()]
                    for hh in range(4):
                        c2 = hh // 2
                        op_, opb = obanks[c2]
                        lo = (hh % 2) * 64
                        for n_, (pt, pb, kind, kb) in enumerate(pts):
                            if kind == "l":
                                vsl = vt[:, (kb % 8) * 256 + kv * 64:(kb % 8) * 256 + kv * 64 + 64]
                                vbuf = VB[kb % 8]
                            else:
                                vsl = vcx[:, kb * 256 + kv * 64: kb * 256 + kv * 64 + 64]
                                vbuf = LB
                            mmk(op_[lo:lo + 64, 0:128], vsl, pt[:, (hh % 2) * 256 + (hh // 2) * 128:(hh % 2) * 256 + (hh // 2) * 128 + 128], n_ == 0, n_ == len(pts) - 1,
                                (pb, vbuf), (opb,))
                    if AST < 5:
                        continue
                    A_("dve", (lambda a: lambda e: e.reciprocal(out=rden[:, :], in_=a))(rsp[:, :]), (rsb,), (RDB,))
                    for hh in range(4):
                        c2 = hh // 2
                        op_, opb = obanks[c2]
                        lo = (hh % 2) * 64
                        A_("dve", (lambda o, a, b: lambda e: e.tensor_tensor(out=o, in0=a, in1=b, op=ALU.mult))(
                            h[lo:lo + 64, 2 * kv + c2, gi * 128:(gi + 1) * 128], op_[lo:lo + 64, 0:128], rden[lo:lo + 64, (hh % 2) * 256 + (hh // 2) * 128:(hh % 2) * 256 + (hh // 2) * 128 + 128]),
                            (opb, RDB), (HB,))

            def group(blocks):
                for gi, i in enumerate(blocks):
                    if ph == "P":
                        s0 = (i // 2) * 2
                        keys = [("l", s0, 0), ("l", s0 + 1, 0)]
                    else:
                        keys = []
                        if i - 1 >= 0:
                            keys.append(("l", i - 1, 1))
                        keys.append(("l", i, 0))
                        if i + 1 < NB:
                            keys.append(("l", i + 1, 2))
                        keys += [("c", 0, 0), ("c", 1, 0)]
                    if "att" in mix:
                        attend(i, keys, gi)
                if "op" in mix:
                    outproj_post(l, r, 1, w_o[j], blocks[0] * 128, len(blocks) * 128, K)

            for t0 in range(0, NT, 512):
                n = min(512, NT - t0)
                project(t0, n)
                b0 = t0 // 128
                if ph == "P":
                    group([b0, b0 + 1, b0 + 2, b0 + 3])
                else:
                    group([b for b in range(b0 - 1, b0 + n // 128 - 1) if b >= 0])
            if ph == "S" and flush:
                group([NB - 1])


        def gelu_tanh(srcs, dst, dstb, n, K, accum=None):
            xs = K["gl"][:, 0:n]
            tt = K["gl"][:, 512:512 + n]
            for (sp_ap, sbuf_, c0, ncol) in srcs:
                A_("act", (lambda o, a: lambda e: e.activation(out=o, in_=a, func=AF.Copy))(K["gl"][:, c0:c0 + ncol], sp_ap), (sbuf_,), (K["glB"][0],))
                A_("act", (lambda o, a: lambda e: e.activation(out=o, in_=a, func=AF.Square))(K["gl"][:, 512 + c0:512 + c0 + ncol], sp_ap), (sbuf_,), (K["glB"][1],))
            A_("dve", lambda e: e.tensor_scalar(out=tt, in0=tt, scalar1=0.044715, scalar2=1.0, op0=ALU.mult, op1=ALU.add), (K["glB"][1],), (K["glB"][1],))
            A_("dve", lambda e: e.tensor_tensor(out=tt, in0=tt, in1=xs, op=ALU.mult), K["glB"], (K["glB"][1],))
            A_("act", lambda e: e.activation(out=tt, in_=tt, func=AF.Sigmoid, scale=1.5957691216057308), (K["glB"][1],), (K["glB"][1],))
            if accum is None:
                A_("dve", lambda e: e.tensor_tensor(out=dst, in0=tt, in1=xs, op=ALU.mult), K["glB"], (dstb,))
            else:
                A_("dve", lambda e: e.scalar_tensor_tensor(out=dst, in0=tt, scalar=1.0, in1=xs, op0=ALU.mult, op1=ALU.mult, accum_out=accum), K["glB"], (dstb,))

        def conv_layer(l, ph, r, NT, flush=True):
            j = l // 2
            NB = NT // 128
            C = Carver()
            K = {}
            GW = 672
            G = C.bf(4 * GW); GB = Buf("G")
            NDG = 16
            dg = C.bf(NDG * 128); DGB = [Buf(f"dg{i}") for i in range(NDG)]
            dgc = {"i": 0}
            BO = C.bf(4 * 640); BOB = Buf("BO")
            vtg = C.bf(4 * 512); VGB = [Buf(f"vg{i}") for i in range(4)]
            sgm = C.f32(2 * 512); SGB = [Buf("sgm0"), Buf("sgm1")]
            K["gl"] = C.f32(2 * 512); K["glB"] = [Buf("gl0"), Buf("gl1")]
            cgb = C.bf(4 * 512); CGB = Buf("cg")
            mu = C.f32(512); var = C.f32(512); rstd = C.f32(512); STB = Buf("stat")
            K["ys"] = C.f32(8 * 512); K["ysB"] = Buf("ys")
            K["sq"] = C.bf(4 * 512); K["sqB"] = [Buf(f"sq{i}") for i in range(4)]
            cgq = K["sq"]
            K["tmp"] = C.f32(2 * 512); K["tmpB"] = [Buf("tmp0"), Buf("tmp1")]
            K["rs"] = C.f32(512); K["rsB"] = Buf("rs")
            slnv = C.f32(1024); convw = C.f32(124); convv = C.f32(12); sguw = C.bf(512); sgub = C.bf(2048); LB = Buf("lconst")
            sts = C.f32(8); SSB = Buf("sts")
            A_("sp", lambda e: e.dma_start(out=slnv, in_=slnv_d[j]), (), (LB,), dma=True)
            A_("sp", lambda e: e.dma_start(out=convw, in_=convw_d[ph][j]), (), (LB,), dma=True)
            A_("sp", lambda e: e.dma_start(out=convv, in_=convv_d[j]), (), (LB,), dma=True)
            A_("pool", lambda e: e.dma_start(out=sguw, in_=sguw_d[ph][j]), (), (LB,), dma=True)
            A_("pool", lambda e: e.dma_start(out=sgub, in_=sgub_d[ph][j]), (), (LB,), dma=True)
            A_("dve", lambda e: e.memset(G, 0.0), (), (GB,))

            def inproj(tok0, n, gsegs, bo0):
                nb = n // 128
                pre(l, r, 1, tok0, n, K)
                for pc in range(2):
                    wv_, wvb = load_wa(w_in[j], pc * 256, 256)
                    wg_, wgb = load_wa(w_in[j], 512 + pc * 256, 256)
                    for ci in range(2):
                        cc = pc * 2 + ci
                        avp, avb = bank()
                        agp, agb = bank()
                        proj_fm(wv_, wvb, ci * 128, n, avp, avb)
                        proj_fm(wg_, wgb, ci * 128, n, agp, agb)
                        q = cc % 2
                        A_("act", (lambda o, a: lambda e: e.activation(out=o, in_=a, func=AF.Sigmoid))(sgm[:, q * 512:q * 512 + n], agp[:, 0:n]), (agb,), (SGB[q],))
                        for (c0, ncol, gu0) in gsegs:
                            A_("dve", (lambda o, a, b: lambda e: e.tensor_tensor(out=o, in0=a, in1=b, op=ALU.mult))(
                                G[:, cc * GW + gu0: cc * GW + gu0 + ncol], avp[:, c0:c0 + ncol], sgm[:, q * 512 + c0:q * 512 + c0 + ncol]), (avb, SGB[q]), (GB,))
                wv0, wv0b = load_wa(w_in[j], 1536, 256)
                wv1, wv1b = load_wa(w_in[j], 1792, 256)
                for bi in range(nb):
                    vp, vpb = bank()
                    vp2, vpb2 = bank()
                    for k in range(8):
                        mmk(vp[:, 0:256], h[:, k, bi * 128:(bi + 1) * 128], wv0[:, k, :], k == 0, k == 7, (HB, wv0b), (vpb,))
                    for k in range(8):
                        mmk(vp2[:, 0:256], h[:, k, bi * 128:(bi + 1) * 128], wv1[:, k, :], k == 0, k == 7, (HB, wv1b), (vpb2,))
                    gq = sgm[:, (bi % 2) * 512:(bi % 2) * 512 + 512]
                    gqb = SGB[bi % 2]
                    gelu_tanh([(vp[:, 0:256], vpb, 0, 256), (vp2[:, 0:256], vpb2, 256, 256)], gq, gqb, 512, K, accum=sts[:, 0:1])
                    A_("dve", lambda e: e.tensor_scalar(out=sts[:, 1:2], in0=sts[:, 0:1], scalar1=-1.0 / 512, scalar2=None, op0=ALU.mult), (gqb,), (SSB,))
                    A_("act", (lambda a: lambda e: e.activation(out=K["gl"][:, 0:512], in_=a, func=AF.Square, bias=sts[:, 1:2], scale=1.0, accum_out=sts[:, 2:3]))(gq), (gqb, SSB), (K["glB"][0], SSB))
                    A_("act", lambda e: e.activation(out=sts[:, 3:4], in_=sts[:, 2:3], func=AF.Sqrt, bias=epsc[:, 0:1], scale=1.0 / 512), (SSB, CONSTB), (SSB,))
                    A_("dve", lambda e: e.reciprocal(out=sts[:, 3:4], in_=sts[:, 3:4]), (SSB,), (SSB,))
                    A_("dve", (lambda a: lambda e: e.tensor_scalar(out=a, in0=a, scalar1=sts[:, 1:2], scalar2=sts[:, 3:4], op0=ALU.add, op1=ALU.mult))(gq), (gqb, SSB), (gqb,))
                    A_("dve", (lambda a: lambda e: e.tensor_tensor(out=a, in0=a, in1=slnv[:, 0:512], op=ALU.mult))(gq), (gqb, LB), (gqb,))
                    A_("dve", (lambda o, a: lambda e: e.tensor_tensor(out=o, in0=a, in1=slnv[:, 512:1024], op=ALU.add))(vtg[:, bi * 512:(bi + 1) * 512], gq), (gqb, LB), (VGB[bi],))
                for pc in range(2):
                    wu_, wub = load_wa(w_in[j], 1024 + pc * 256, 256)
                    for ci in range(2):
                        g = pc * 2 + ci
                        up, upb = bank()
                        proj_fm(wu_, wub, ci * 128, n, up, upb)
                        ug = sgm[:, (g % 2) * 512:(g % 2) * 512 + n]
                        ugb = SGB[g % 2]
                        gelu_tanh([(up[:, 0:n], upb, 0, n)], ug, ugb, n, K)
                        spp, sppb = bank()
                        mmk(spp[:, 0:n], e0[:, :], sgub[:, g * 512: g * 512 + n], True, False, (ONESB, LB), (sppb,))
                        for bi in range(nb):
                            mmk(spp[:, bi * 128:(bi + 1) * 128], vtg[:, bi * 512 + g * 128: bi * 512 + g * 128 + 128], sguw[:, g * 128:(g + 1) * 128], False, bi == nb - 1, (VGB[bi], LB), (sppb,))
                        A_("dve", (lambda o, a, b: lambda e: e.tensor_tensor(out=o, in0=a, in1=b, op=ALU.mult))(
                            BO[:, g * 640 + bo0: g * 640 + bo0 + n], spp[:, 0:n], ug), (sppb, ugb), (BOB,))

            def group(tokg0, n, csegs, bo0):
                cps = [bank() for cc in range(4)]
                assert len(csegs) == 1
                (c0, ncol, gu0) = csegs[0]
                for cc in range(4):
                    cp, cpb = cps[cc]
                    for k in range(31):
                        di = dgc["i"]
                        dgc["i"] = (di + 1) % NDG
                        dsl = dg[:, di * 128:(di + 1) * 128]
                        A_("pool", (lambda o, w: lambda e: e.tensor_scalar(out=o, in0=ident[:, :], scalar1=w, scalar2=0.0, op0=ALU.mult, op1=ALU.add))(
                            dsl, convw[:, cc * 31 + k: cc * 31 + k + 1]), (LB, CONSTB), (DGB[di],))
                        src = G[:, cc * GW + gu0 - 15 + k: cc * GW + gu0 - 15 + k + ncol]
                        mmk(cp[:, c0:c0 + ncol], dsl, src, k == 0, k == 30, (DGB[di], GB), (cpb,))
                for cc in range(4):
                    cp, cpb = cps[cc]
                    A_("act", (lambda o, a, b: lambda e: e.activation(out=o, in_=a, func=AF.Identity, bias=b, scale=1.0))(cgb[:, cc * 512:cc * 512 + n], cp[:, 0:n], convv[:, cc:cc + 1]), (cpb, LB), (CGB,))
                    A_("act", (lambda o, a, b: lambda e: e.activation(out=o, in_=a, func=AF.Square, bias=b, scale=1.0))(cgq[:, cc * 512:cc * 512 + n], cp[:, 0:n], convv[:, cc:cc + 1]), (cpb, LB), (K["sqB"][cc],))
                s1, s1b = sbank()
                for cc in range(4):
                    mmk(s1[:, 0:n], ones[:, :], cgb[:, cc * 512:cc * 512 + n], cc == 0, cc == 3, (CGB, ONESB), (s1b,))
                s2, s2b = sbank()
                for cc in range(4):
                    mmk(s2[:, 0:n], ones[:, :], cgq[:, cc * 512:cc * 512 + n], cc == 0, cc == 3, (K["sqB"][cc], ONESB), (s2b,))
                A_("dve", lambda e: e.tensor_scalar(out=mu[:, 0:n], in0=s1[:, 0:n], scalar1=1.0 / 512, scalar2=None, op0=ALU.mult), (s1b,), (STB,))
                A_("dve", lambda e: e.tensor_tensor(out=var[:, 0:n], in0=mu[:, 0:n], in1=mu[:, 0:n], op=ALU.mult), (STB,), (STB,))
                A_("dve", lambda e: e.scalar_tensor_tensor(out=var[:, 0:n], in0=s2[:, 0:n], scalar=1.0 / 512, in1=var[:, 0:n], op0=ALU.mult, op1=ALU.subtract), (s2b, STB), (STB,))
                A_("act", lambda e: e.activation(out=rstd[:, 0:n], in_=var[:, 0:n], func=AF.Sqrt, bias=epsc[:, 0:1], scale=1.0), (STB, CONSTB), (STB,))
                A_("dve", lambda e: e.reciprocal(out=rstd[:, 0:n], in_=rstd[:, 0:n]), (STB,), (STB,))
                for cc in range(4):
                    cp, cpb = cps[cc]
                    q = cc % 2
                    lt = K["tmp"][:, q * 512:q * 512 + n]
                    A_("dve", (lambda o, a, b: lambda e: e.scalar_tensor_tensor(out=o, in0=a, scalar=b, in1=mu[:, 0:n], op0=ALU.add, op1=ALU.subtract))(lt, cp[:, 0:n], convv[:, cc:cc + 1]), (cpb, LB, STB), (K["tmpB"][q],))
                    A_("dve", (lambda o: lambda e: e.tensor_tensor(out=o, in0=o, in1=rstd[:, 0:n], op=ALU.mult))(lt), (K["tmpB"][q], STB), (K["tmpB"][q],))
                    A_("act", (lambda o, a, sc, b: lambda e: e.activation(out=o, in_=a, func=AF.Silu, bias=b, scale=sc))(h[:, cc, 0:n], lt, convv[:, 4 + cc:5 + cc], convv[:, 8 + cc:9 + cc]), (K["tmpB"][q], LB), (HB,))
                    A_("act", (lambda o, a: lambda e: e.activation(out=o, in_=a, func=AF.Copy))(h[:, 4 + cc, 0:n], BO[:, cc * 640 + bo0: cc * 640 + bo0 + n]), (BOB,), (HB,))
                outproj_post(l, r, 1, w_out[j], tokg0, n, K)

            if "grp" not in mix:
                group = lambda *a: None
            if ph == "P":
                for t0 in range(0, NT, 512):
                    inproj(t0, 512, [(0, 256, 16), (256, 256, 304)], 0)
                    group(t0, 256, [(0, 256, 16)], 0)
                    group(t0 + 256, 256, [(0, 256, 304)], 256)
            else:
                for t0 in range(0, NT, 512):
                    n = min(512, NT - t0)
                    if t0 > 0:
                        for cc in range(4):
                            A_("act", (lambda o, a: lambda e: e.activation(out=o, in_=a, func=AF.Copy))(G[:, cc * GW: cc * GW + 144], G[:, cc * GW + 512: cc * GW + 656]), (GB,), (GB,))
                            A_("act", (lambda o, a: lambda e: e.activation(out=o, in_=a, func=AF.Copy))(BO[:, cc * 640: cc * 640 + 128], BO[:, cc * 640 + 512: cc * 640 + 640]), (BOB,), (BOB,))
                    inproj(t0, n, [(0, n, 144)], 128)
                    if t0 == 0:
                        group(0, n - 128, [(0, n - 128, 144)], 128)
                    else:
                        group(t0 - 128, n, [(0, n, 16)], 0)
                if flush:
                    assert NT % 512 == 0
                    group(NT - 128, 128, [(0, 128, 528)], 512)

        for ph in phases:
            NT = NTP if ph == "P" else NTS
            r = 0 if ph == "P" else 1
            for c in range(8):
                P.add("sp", (lambda o, a: lambda e: e.dma_start(out=o, in_=a))(x[:, c, 0:NT], xT[ph][c * 128:(c + 1) * 128, :]),
                      (), XB[0:NT // 128], dma=True)
            def mk_tiles(ntok):
                nb_ = ntok // 128
                nt_ = (nb_ + 3) // 4
                base_, rem_ = nb_ // nt_, nb_ % nt_
                sizes = [base_ + 1] * rem_ + [base_] * (nt_ - rem_)
                out, t0 = [], 0
                for sz in sizes:
                    out.append((t0, sz * 128))
                    t0 += sz * 128
                return out
            for l in range(depth):
                shrink = (ph == "S") and SHRINK and depth == DEPTH
                nt1 = NT - 128 * l if shrink else NT
                nt2 = nt1 - 128 if shrink else NT
                K = ffn_carve()
                prepre["next"] = (l, 1, min(512, nt1)) if (do_mixer and PREPRE) else None
                ffn_sublayer(l, 0, r, mk_tiles(nt1), K)
                P.barrier()
                if do_mixer:
                    if l % 2 == 1:
                        if "attn" in mix:
                            attn_layer(l, ph, r, nt1, flush=not shrink)
                    else:
                        if "conv" in mix:
                            conv_layer(l, ph, r, nt1, flush=not shrink)
                    P.barrier()
                K = ffn_carve()
                nxt1 = (NT - 128 * (l + 1)) if shrink else NT
                prepre["next"] = (l + 1, 0, min(512, nxt1)) if (l + 1 < depth and PREPRE) else None
                ffn_sublayer(l, 1, r, mk_tiles(nt2), K)
                P.barrier()
            NO = NTP if ph == "P" else NOWN
            for c in range(8):
                P.add("sp", (lambda o, a: lambda e: e.dma_start(out=o, in_=a))(yT[ph][c * 128:(c + 1) * 128, :], x[:, c, 0:NO]),
                      XB[0:NO // 128], (), dma=True)
            P.barrier()

        cnt = P.finalize()
        print("ops", len(P.ops), "signals", cnt, "dmas", P.ndma)
        with nc.Block() as block:
            @block.tensor
            def _(e):
                P.emit("pe", e, esem, dsem)

            @block.scalar
            def _(e):
                P.emit("act", e, esem, dsem)

            @block.vector
            def _(e):
                P.emit("dve", e, esem, dsem)

            @block.gpsimd
            def _(e):
                P.emit("pool", e, esem, dsem)

            @block.sync
            def _(e):
                P.emit("sp", e, esem, dsem)
    return nc


_PERM = np.concatenate([np.arange(16, 32), np.arange(0, 16), np.arange(48, 64), np.arange(32, 48)])


def _fm(v, nch):
    return np.ascontiguousarray(np.asarray(v, np.float32).reshape(nch, 128).T)


def _prep_shared(inp):
    sh = {}
    for k in ("w_mod", "ffn_w_gate", "ffn_w_up", "ffn_w_down", "cm_w_in", "cm_w_out", "attn_w_o"):
        sh[k] = np.ascontiguousarray(inp[k], dtype=np.float32)
    wq = np.asarray(inp["attn_w_qkv"], np.float32)
    blocks = []
    for j in range(2):
        q = wq[j][:, :1024]
        k = wq[j][:, 1024:1280]
        qperm = q.reshape(1024, 16, 64)[:, :, _PERM].reshape(1024, 1024)
        k4 = k.reshape(1024, 4, 64)
        kdup = np.concatenate([k4, k4], axis=2).reshape(1024, 512)
        kp4 = k4[:, :, _PERM]
        kpdup = np.concatenate([kp4, kp4], axis=2).reshape(1024, 512)
        blocks.append(np.concatenate([q, qperm, kdup, kpdup, wq[j][:, 1024:1536]], axis=1))
    sh["w_att"] = np.ascontiguousarray(np.stack(blocks))
    bm = np.asarray(inp["b_mod"], np.float32)
    sh["bmodT"] = np.ascontiguousarray(np.concatenate([_fm(bm[l], 72) for l in range(DEPTH)], axis=1))
    nw = np.asarray(inp["norm_w"], np.float32)
    sh["normwT"] = np.ascontiguousarray(np.concatenate([_fm(nw[l].reshape(-1), 48) for l in range(DEPTH)], axis=1))
    cv = []
    sl = []
    for j in range(2):
        cv.append(np.concatenate([_fm(inp[k][j], 4) for k in ("cm_conv_b", "cm_conv_ln_g", "cm_conv_ln_b")], axis=1))
        sl.append(np.concatenate([np.tile(np.asarray(inp[k][j], np.float32)[None, :], (128, 1))
                                  for k in ("cm_sgu_ln_g", "cm_sgu_ln_b")], axis=1))
    sh["convv"] = np.ascontiguousarray(np.stack(cv))
    sh["slnv"] = np.ascontiguousarray(np.stack(sl))
    sk = np.asarray(inp["attn_sink"], np.float32)
    sk = sk.reshape(2, 4, 4)[:, :, [0, 2, 1, 3]].reshape(2, 16)
    sh["sinkrow"] = np.ascontiguousarray(np.tile(np.repeat(sk, 128, axis=1).reshape(2, 1, 2048), (1, 128, 1)))
    a = np.arange(128)
    mprev = (a[None, :] <= a[:, None]).astype(np.float32)
    mnext = (a[:, None] <= a[None, :]).astype(np.float32)
    sh["ident"] = np.ascontiguousarray(np.eye(128, dtype=np.float32))
    sh["masks"] = np.ascontiguousarray(np.concatenate([np.tile(mprev, (1, 4)), np.tile(mnext, (1, 4))], axis=1))
    return sh


def _conv_sgu(inp, rev):
    cw = np.asarray(inp["cm_conv_w"], np.float32)
    sw = np.asarray(inp["cm_sgu_w"], np.float32)
    sb_ = np.asarray(inp["cm_sgu_b"], np.float32)
    if rev:
        cw = cw[:, ::-1, :]
        sw = sw[:, :, ::-1, ::-1]
        sb_ = sb_[:, :, ::-1]
    convw = np.stack([np.ascontiguousarray(cw[j].T.reshape(4, 128, 31).transpose(1, 0, 2)).reshape(128, 124) for j in range(2)])
    sguw = np.stack([np.ascontiguousarray(sw[j].transpose(2, 0, 1)).reshape(128, 512) for j in range(2)])
    sgub = np.ascontiguousarray(np.tile(np.tile(sb_.reshape(2, 4, 1, 128), (1, 1, 4, 1)).reshape(2, 1, 2048), (1, 128, 1)))
    return np.ascontiguousarray(convw), np.ascontiguousarray(sguw), sgub


def _rope_tables(rev):
    t = np.arange(NTS)
    g = (4095 - t) if rev else t
    row = (g // 64).astype(np.float32)
    col = (g % 64).astype(np.float32)
    invf = np.power(np.float32(10000.0), -np.arange(0, 32, 2, dtype=np.float32) / np.float32(32)).astype(np.float32)
    cs = np.zeros((128, 2, NTS), np.float32)
    for p in range(128):
        d = p % 64
        if d < 32:
            ang = row * invf[d % 16]
            sign = -1.0 if d < 16 else 1.0
        else:
            ang = col * invf[(d - 32) % 16]
            sign = -1.0 if (d - 32) < 16 else 1.0
        ang = ang.astype(np.float32)
        cs[p, 0] = np.cos(ang)
        cs[p, 1] = sign * np.sin(ang)
    return np.ascontiguousarray(cs.reshape(128, 2 * NTS))


def _prep_core(inp, sh, i, cache):
    b, half = i // 2, i % 2
    m = dict(sh)
    xp = np.asarray(inp["x_prompt"], np.float32)[4 * i:4 * i + 4].reshape(NTP, D)
    m["xpT"] = np.ascontiguousarray(xp.T)
    xs = np.asarray(inp["x_sample"], np.float32)[b]
    xs = xs[::-1][:NTS] if half else xs[:NTS]
    m["xsT"] = np.ascontiguousarray(xs.T)
    cond = np.stack([_fm(inp["c_ctx"], 8), _fm(inp["c"][b], 8)], axis=2).reshape(128, 16)
    m["condT"] = np.ascontiguousarray(cond)
    for rev in (0, 1):
        if ("cs", rev) not in cache:
            cache[("cs", rev)] = _conv_sgu(inp, rev), _rope_tables(rev)
    (cw_p, sw_p, sb_p), _ = cache[("cs", 0)]
    (cw_s, sw_s, sb_s), cs = cache[("cs", half)]
    m.update(convw_p=cw_p, sguw_p=sw_p, sgub_p=sb_p, convw_s=cw_s, sguw_s=sw_s, sgub_s=sb_s, cs=cs)
    ck = np.asarray(inp["cache_k"], np.float32)[b]
    cvv = np.asarray(inp["cache_v"], np.float32)[b]
    kt = ck.transpose(0, 3, 2, 1)
    m["kcT"] = np.ascontiguousarray(np.concatenate([kt, kt], axis=1).reshape(2, 128, 1024))
    m["vc"] = np.ascontiguousarray(cvv.reshape(2, 2, 128, 256).transpose(0, 2, 1, 3).reshape(2, 128, 512))
    return m


_NC_CACHE = {}


def kernel(**inputs):
    key = "full"
    if key not in _NC_CACHE:
        _NC_CACHE[key] = build()
    nc = _NC_CACHE[key]
    sh = _prep_shared(inputs)
    cache = {}
    in_maps = [_prep_core(inputs, sh, i, cache) for i in range(8)]
    res = run_bass_kernel_spmd(nc, in_maps, core_ids=list(range(8)))
    yp = np.zeros((32, 256, D), np.float32)
    ys = np.zeros((4, 4096, D), np.float32)
    nk = np.zeros((32, 2, 256, 4, 64), np.float32)
    nv = np.zeros((32, 2, 256, 4, 64), np.float32)
    for i in range(8):
        r = res.results[i]
        b, half = i // 2, i % 2
        yp[4 * i:4 * i + 4] = np.asarray(r["ypT"]).T.reshape(4, 256, D)
        o = np.asarray(r["ysT"]).T
        if half:
            ys[b, 2048:] = o[::-1]
        else:
            ys[b, :2048] = o
        nk[4 * i:4 * i + 4] = np.asarray(r["nk"]).reshape(2, 4, 256, 4, 64).transpose(1, 0, 2, 3, 4)
        nv[4 * i:4 * i + 4] = np.asarray(r["nv"]).reshape(2, 4, 256, 4, 64).transpose(1, 0, 2, 3, 4)
    return yp, ys, nk, nv
```
